# Optimizing a Trainium2 kernel written in Bass

```python
import jax, jax.numpy as jnp
from jax import lax
import numpy as np

D_MODEL = 2048
BATCH = 4
SEQ = 4096
DEPTH = 1

HEAD_DIM = 128
ROPE_THETA = 10000.0
BLOCK = 128
A_PATTERNS = ((128, 1), (512, 4), (2048, 16))
A_HEADS_PER_GROUP = 4
A_GROUPS = len(A_PATTERNS)
A_HEADS = A_HEADS_PER_GROUP * A_GROUPS
A_QKV = A_HEADS * HEAD_DIM
A_OUT = A_HEADS_PER_GROUP * HEAD_DIM
B_Q_HEADS = 8
B_KV_HEADS = 2
B_GROUP = B_Q_HEADS // B_KV_HEADS
B_WINDOW = 128
B_Q = B_Q_HEADS * HEAD_DIM
B_KV = B_KV_HEADS * HEAD_DIM
IN_COLS = 3 * A_QKV + B_Q + 2 * B_KV + 2 * D_MODEL
PEER_HEADS = 8
PEER_QDIM = 256
PEER_HALF = PEER_QDIM // 2
N_KEYS = 128
N_EXPERTS = N_KEYS * N_KEYS
PEER_TOPK = 16
PEER_CHUNK = 128
PLE_DIM = 256
LN_EPS = 1e-5
DN_ALPHA = (2 * DEPTH) ** 0.25
DN_BETA = (8 * DEPTH) ** -0.25

kernel_name = "hybrid_dilated_swa_peer_deepnorm"


def layer_norm(x, g, b):
    xf = x.astype(jnp.float32)
    mu = jnp.mean(xf, axis=-1, keepdims=True)
    var = jnp.mean(jnp.square(xf - mu), axis=-1, keepdims=True)
    return ((xf - mu) * lax.rsqrt(var + LN_EPS) * g.astype(jnp.float32) + b.astype(jnp.float32)).astype(x.dtype)


def rope_tables(seq):
    inv = 1.0 / (ROPE_THETA ** (jnp.arange(0, HEAD_DIM, 2, dtype=jnp.float32) / HEAD_DIM))
    ang = jnp.arange(seq, dtype=jnp.float32)[:, None] * inv[None, :]
    return jnp.cos(ang), jnp.sin(ang)


def apply_rope(t, cos, sin):
    t1, t2 = jnp.split(t.astype(jnp.float32), 2, axis=-1)
    c = cos[:, None, :]
    s = sin[:, None, :]
    return jnp.concatenate([t1 * c - t2 * s, t1 * s + t2 * c], axis=-1).astype(t.dtype)


def banded_causal_attention(q, k, v, max_dist, sink=None):
    L, hd = q.shape[-2], q.shape[-1]
    nb = -(-L // BLOCK)
    pad = nb * BLOCK - L
    if pad:
        q = jnp.pad(q, [(0, 0)] * (q.ndim - 2) + [(0, pad), (0, 0)])
        k = jnp.pad(k, [(0, 0)] * (k.ndim - 2) + [(0, pad), (0, 0)])
        v = jnp.pad(v, [(0, 0)] * (v.ndim - 2) + [(0, pad), (0, 0)])
    lead = k.shape[:-2]
    qb = q.reshape(*q.shape[:-2], nb, BLOCK, hd)

    def band(t):
        tb = t.reshape(*lead, nb, BLOCK, hd)
        prev = jnp.concatenate([jnp.zeros_like(tb[..., :1, :, :]), tb[..., :-1, :, :]], axis=-3)
        return jnp.concatenate([prev, tb], axis=-2)

    kb, vb = band(k), band(v)
    s = jnp.einsum('...gnqd,...nkd->...gnqk', qb, kb,
                   preferred_element_type=jnp.float32) * (hd ** -0.5)
    qpos = jnp.arange(nb)[:, None, None] * BLOCK + jnp.arange(BLOCK)[None, :, None]
    kpos = jnp.arange(nb)[:, None, None] * BLOCK - BLOCK + jnp.arange(2 * BLOCK)[None, None, :]
    dist = qpos - kpos
    mask = (dist >= 0) & (dist <= max_dist) & (kpos >= 0)
    s = jnp.where(mask, s, -jnp.inf)
    m = jnp.max(s, axis=-1)
    if sink is not None:
        m = jnp.maximum(m, sink)
    pr = jnp.exp(s - m[..., None])
    l = jnp.sum(pr, axis=-1)
    denom = l + jnp.exp(sink - m) if sink is not None else l
    o = jnp.einsum('...gnqk,...nkd->...gnqd', pr, vb.astype(jnp.float32)) / denom[..., None]
    o = o.reshape(*o.shape[:-3], nb * BLOCK, hd)[..., :L, :]
    m = m.reshape(*m.shape[:-2], nb * BLOCK)[..., :L]
    l = l.reshape(*l.shape[:-2], nb * BLOCK)[..., :L]
    return o, m, l


def dilated_group_attention(q, k, v, window, dil):
    Bn, H, S, hd = q.shape
    L = S // dil

    def gather(t):
        return t.reshape(Bn, H, L, dil, hd).transpose(0, 1, 3, 2, 4)

    o, m, l = banded_causal_attention(gather(q)[..., None, :, :], gather(k), gather(v), window // dil)
    o = o[..., 0, :, :].transpose(0, 1, 3, 2, 4).reshape(Bn, H, S, hd)
    m = m[..., 0, :].transpose(0, 1, 3, 2).reshape(Bn, H, S)
    l = l[..., 0, :].transpose(0, 1, 3, 2).reshape(Bn, H, S)
    return o, m, l


def hybrid_mixer(h, cos, sin, w_in, sinks, w_branch_a, w_branch_b, w_out):
    Bn, S, _ = h.shape
    z = h @ w_in
    cuts = [int(c) for c in np.cumsum([A_QKV, A_QKV, A_QKV, B_Q, B_KV, B_KV, D_MODEL])]
    qa, ka, va, qb, kb, vb, ga, gb = jnp.split(z, cuts, axis=-1)

    def heads_a(t):
        t = t.reshape(Bn, S, A_GROUPS, A_HEADS_PER_GROUP, HEAD_DIM)
        return t.transpose(2, 0, 3, 1, 4)
    qa = heads_a(apply_rope(qa.reshape(Bn, S, A_HEADS, HEAD_DIM), cos, sin))
    ka = heads_a(apply_rope(ka.reshape(Bn, S, A_HEADS, HEAD_DIM), cos, sin))
    va = heads_a(va)
    os_, ms_, ls_ = [], [], []
    for g, (window, dil) in enumerate(A_PATTERNS):
        o, m, l = dilated_group_attention(qa[g], ka[g], va[g], window, dil)
        os_.append(o)
        ms_.append(m)
        ls_.append(l)
    os_, ms_, ls_ = jnp.stack(os_), jnp.stack(ms_), jnp.stack(ls_)
    wts = ls_ * jnp.exp(ms_ - jnp.max(ms_, axis=0, keepdims=True))
    o_a = jnp.sum(wts[..., None] * os_, axis=0) / jnp.sum(wts, axis=0)[..., None]
    o_a = o_a.transpose(0, 2, 1, 3).reshape(Bn, S, A_OUT).astype(h.dtype)

    qb = apply_rope(qb.reshape(Bn, S, B_Q_HEADS, HEAD_DIM), cos, sin)
    qb = qb.reshape(Bn, S, B_KV_HEADS, B_GROUP, HEAD_DIM).transpose(0, 2, 3, 1, 4)
    kb = apply_rope(kb.reshape(Bn, S, B_KV_HEADS, HEAD_DIM), cos, sin).transpose(0, 2, 1, 3)
    vb = vb.reshape(Bn, S, B_KV_HEADS, HEAD_DIM).transpose(0, 2, 1, 3)
    sink = sinks.astype(jnp.float32).reshape(B_KV_HEADS, B_GROUP, 1, 1)
    o_b, _, _ = banded_causal_attention(qb, kb, vb, B_WINDOW - 1, sink)
    o_b = o_b.transpose(0, 3, 1, 2, 4).reshape(Bn, S, B_Q).astype(h.dtype)

    merged = jax.nn.sigmoid(ga) * (o_a @ w_branch_a) + jax.nn.sigmoid(gb) * (o_b @ w_branch_b)
    return merged @ w_out


def peer_ffn(h, wq, subkeys, u_tab, v_tab):
    Bn, S, D = h.shape
    T = Bn * S
    xt = h.reshape(T, D)
    q = (xt @ wq).reshape(T, PEER_HEADS, 2, PEER_HALF)
    s = jnp.einsum('thcd,hcnd->thcn', q, subkeys, preferred_element_type=jnp.float32)
    top_s, top_i = lax.top_k(s, PEER_TOPK)
    cand = top_s[:, :, 0, :, None] + top_s[:, :, 1, None, :]
    best_s, best_c = lax.top_k(cand.reshape(T, PEER_HEADS, PEER_TOPK * PEER_TOPK), PEER_TOPK)
    i1 = jnp.take_along_axis(top_i[:, :, 0], best_c // PEER_TOPK, axis=-1)
    i2 = jnp.take_along_axis(top_i[:, :, 1], best_c % PEER_TOPK, axis=-1)
    expert = i1 * N_KEYS + i2
    gate = jax.nn.softmax(best_s, axis=-1)
    nc = T // PEER_CHUNK

    def chunk(args):
        xc, ec, gc = args
        u = u_tab[ec]
        act = jax.nn.gelu(jnp.einsum('cd,chkd->chk', xc, u, preferred_element_type=jnp.float32),
                          approximate=False)
        w = (gc * act).astype(v_tab.dtype)
        return jnp.einsum('chk,chkd->cd', w, v_tab[ec])

    out = lax.map(chunk, (xt.reshape(nc, PEER_CHUNK, D),
                          expert.reshape(nc, PEER_CHUNK, PEER_HEADS, PEER_TOPK),
                          gate.reshape(nc, PEER_CHUNK, PEER_HEADS, PEER_TOPK)))
    return out.reshape(Bn, S, D).astype(h.dtype)


def setup_inputs(seed: int = 0) -> dict:
    key = jax.random.key(seed)
    ks = jax.random.split(key, 18)
    f = jnp.float32

    def nrm(k, shape, scale):
        return jax.random.normal(k, shape, f) * scale

    return {
        "x": nrm(ks[0], (BATCH, SEQ, D_MODEL), 1.0),
        "p": nrm(ks[1], (DEPTH, BATCH, SEQ, PLE_DIM), 1.0),
        "w_in": nrm(ks[2], (DEPTH, D_MODEL, IN_COLS), D_MODEL ** -0.5),
        "sinks": nrm(ks[3], (DEPTH, B_Q_HEADS), 0.5),
        "w_branch_a": nrm(ks[4], (DEPTH, A_OUT, D_MODEL), A_OUT ** -0.5),
        "w_branch_b": nrm(ks[5], (DEPTH, B_Q, D_MODEL), B_Q ** -0.5),
        "w_out": nrm(ks[6], (DEPTH, D_MODEL, D_MODEL), DN_BETA * D_MODEL ** -0.5),
        "ln1_g": 1.0 + nrm(ks[7], (DEPTH, D_MODEL), 0.02),
        "ln1_b": nrm(ks[8], (DEPTH, D_MODEL), 0.02),
        "peer_wq": nrm(ks[9], (DEPTH, D_MODEL, PEER_HEADS * PEER_QDIM), D_MODEL ** -0.5),
        "peer_subkeys": nrm(ks[10], (DEPTH, PEER_HEADS, 2, N_KEYS, PEER_HALF), PEER_HALF ** -0.5),
        "peer_u": nrm(ks[11], (DEPTH, N_EXPERTS, D_MODEL), D_MODEL ** -0.5),
        "peer_v": nrm(ks[12], (DEPTH, N_EXPERTS, D_MODEL), DN_BETA * PEER_HEADS ** -0.5),
        "ple_gate": nrm(ks[13], (DEPTH, D_MODEL, D_MODEL), D_MODEL ** -0.5),
        "ple_proj": nrm(ks[14], (DEPTH, PLE_DIM, D_MODEL), DN_BETA * PLE_DIM ** -0.5),
        "ln2_g": 1.0 + nrm(ks[15], (DEPTH, D_MODEL), 0.02),
        "ln2_b": nrm(ks[16], (DEPTH, D_MODEL), 0.02),
    }


def reference(x, p, w_in, sinks, w_branch_a, w_branch_b, w_out, ln1_g, ln1_b,
              peer_wq, peer_subkeys, peer_u, peer_v, ple_gate, ple_proj, ln2_g, ln2_b):
    cos, sin = rope_tables(x.shape[1])
    for i in range(DEPTH):
        mix = hybrid_mixer(x, cos, sin, w_in[i], sinks[i], w_branch_a[i], w_branch_b[i], w_out[i])
        h1 = layer_norm(DN_ALPHA * x + mix, ln1_g[i], ln1_b[i])
        ffn = peer_ffn(h1, peer_wq[i], peer_subkeys[i], peer_u[i], peer_v[i])
        ple = jax.nn.sigmoid(h1 @ ple_gate[i]) * (p[i] @ ple_proj[i])
        x = layer_norm(DN_ALPHA * h1 + ffn + ple, ln2_g[i], ln2_b[i])
    return x
```

```python
import math
from contextlib import ExitStack

import ml_dtypes
import numpy as np

import concourse.bass as bass
import concourse.mybir as mybir
from concourse.bass_utils import run_bass_kernel_spmd

F32 = mybir.dt.float32
BF16 = mybir.dt.bfloat16
U32 = mybir.dt.uint32
AF = mybir.ActivationFunctionType
ALU = mybir.AluOpType
AX = mybir.AxisListType

ENG_NAMES = ["pe", "act", "dve", "pool", "sp"]
N_DMA_SEMS = 56
N_SW_SEMS = 16


class Op:
    __slots__ = ("eng", "fn", "deps", "signal", "is_dma", "dma_slot", "dma_val", "sigval")

    def __init__(self, eng, fn, is_dma):
        self.eng = eng
        self.fn = fn
        self.deps = []
        self.signal = False
        self.is_dma = is_dma
        self.dma_slot = None
        self.dma_val = None
        self.sigval = None


class Sched:
    def __init__(self):
        self.ops = {e: [] for e in ENG_NAMES}
        self.last_w = {}
        self.readers = {}
        self.n_dma = 0
        self.n_dma_sw = 0
        self.dma_hist = {}
        self.finals = []
        self._fin = False

    def add(self, eng, fn, reads=(), writes=(), dma=False):
        def flat(ks):
            o = []
            for k_ in ks:
                if isinstance(k_, list):
                    o.extend(k_)
                else:
                    o.append(k_)
            return o
        reads = flat(reads)
        writes = flat(writes)
        op = Op(eng, fn, dma)
        deps = []
        for r in reads:
            w = self.last_w.get(r)
            if w is not None:
                deps.append(w)
        for r in writes:
            w = self.last_w.get(r)
            if w is not None:
                deps.append(w)
            deps.extend(self.readers.get(r, ()))
        if dma:
            if eng == "pool":
                k = self.n_dma_sw
                self.n_dma_sw += 1
                op.dma_slot = k % N_SW_SEMS
                op.dma_val = 16 * (k // N_SW_SEMS + 1)
            else:
                k = self.n_dma
                self.n_dma += 1
                op.dma_slot = N_SW_SEMS + k % (N_DMA_SEMS - N_SW_SEMS)
                op.dma_val = 16 * (k // (N_DMA_SEMS - N_SW_SEMS) + 1)
            prev = self.dma_hist.get(op.dma_slot)
            if prev is not None:
                deps.append(prev)
            self.dma_hist[op.dma_slot] = op
        fp = getattr(self, "fence_pending", None)
        if fp and fp.get(eng):
            deps.extend(fp[eng])
            fp[eng] = []
        seen = set()
        for d in deps:
            if d is op or id(d) in seen:
                continue
            seen.add(id(d))
            if d.eng == "pe" and eng == "pe" and not d.is_dma and not dma:
                continue
            op.deps.append(d)
            d.signal = True
        self.ops[eng].append(op)
        for r in reads:
            self.readers.setdefault(r, []).append(op)
        for r in writes:
            self.last_w[r] = op
            self.readers[r] = []
        return op

    def fence(self):
        deps = []
        for e in ENG_NAMES:
            last = None
            for op in reversed(self.ops[e]):
                if not op.is_dma:
                    last = op
                    break
            if last is not None:
                deps.append(last)
        deps.extend(self.dma_hist.values())
        self.fence_pending = {e: list(deps) for e in ENG_NAMES}

    def finalize(self):
        if self._fin:
            return
        self._fin = True
        for e in ENG_NAMES:
            c = 0
            for op in self.ops[e]:
                if op.is_dma:
                    continue
                if op.signal:
                    c += 1
                    op.sigval = c

    def emit_engine(self, e, h, sems, dma_sems):
        self.finalize()
        waited = {}

        def wait_for(d):
            if d.is_dma:
                key = ("dma", d.dma_slot)
                val = d.dma_val
                sem = dma_sems[d.dma_slot]
            else:
                key = d.eng
                val = d.sigval
                sem = sems[d.eng]
            if waited.get(key, 0) >= val:
                return
            waited[key] = val
            h.wait_ge(sem, val)

        for op in self.ops[e]:
            for d in op.deps:
                wait_for(d)
            ins = op.fn(h)
            if op.is_dma:
                ins.then_inc(dma_sems[op.dma_slot], 16)
            elif op.signal:
                ins.then_inc(sems[e], 1)
        if e == "sp":
            for d in self.finals:
                wait_for(d)


HD = 128
TOWN = 2048
HALO = 2048
CTX = TOWN + HALO
A_PATTERNS = ((128, 1), (512, 4), (2048, 16))
N_EXP = 16384
PLE = 256
LN_EPS = 1e-5


def build_program(D, depth_alpha, n_super=None, peer_slots=128):
    KC = D // 128
    NDG = D // 512 if D >= 512 else 1
    DG = min(512, D)
    IN_COLS = 6144 + 2 * D
    TS = 512
    TS2 = 256
    NSUB = TS2 // 128
    NSUP = TOWN // TS2 if n_super is None else n_super
    alpha = float(depth_alpha)
    scale = 1.0 / math.sqrt(HD)

    nc = bass.Bass("TRN2", target_bir_lowering=False)

    def din(name, shape, dt=F32):
        return nc.dram_tensor(name, list(shape), dt, kind="ExternalInput").ap()

    def dscr(name, shape, dt):
        return nc.dram_tensor(name, list(shape), dt, kind="Internal").ap()

    xo = din("xo", [TOWN, D])
    xh = din("xh", [HALO, D])
    pin = din("p", [TOWN, PLE])
    w_in = din("w_in", [D, IN_COLS])
    sinks = din("sinks", [1, 8])
    w_ba = din("w_branch_a", [512, D])
    w_bb = din("w_branch_b", [1024, D])
    w_out = din("w_out", [D, D])
    ln1_g = din("ln1_g", [1, D])
    ln1_b = din("ln1_b", [1, D])
    peer_wq = din("peer_wq", [D, 2048])
    peer_sk = din("peer_subkeys", [16, 128, 128])
    peer_u = din("peer_u", [N_EXP, D])
    peer_v = din("peer_v", [N_EXP, D])
    ple_gate = din("ple_gate", [D, D])
    ple_proj = din("ple_proj", [PLE, D])
    ln2_g = din("ln2_g", [1, D])
    ln2_b = din("ln2_b", [1, D])
    c_identf = din("c_identf", [128, 128])
    c_bf = din("c_bf", [128, 3, 128], BF16)
    c_maskA = din("c_maskA", [128, 3, 128], BF16)
    c_maskB = din("c_maskB", [128, 3, 512], BF16)
    c_cs = din("c_cs", [128, 2, CTX], BF16)
    c_iota = din("c_iota", [128, 2, 16])

    out = nc.dram_tensor("out", [TOWN, D], F32, kind="ExternalOutput").ap()

    QT = dscr("QT", [20, 128, TOWN], BF16)
    KT = dscr("KT", [14, 128, CTX], BF16)
    VT = dscr("VT", [CTX, 14, 128], BF16)
    OT = dscr("OT", [12, 128, TOWN], BF16)
    Y2 = dscr("Y2", [TOWN, D], F32)
    UVB = dscr("UVB", [N_EXP, 2, D], BF16)
    H1s = dscr("H1s", [TOWN, D], F32)
    QPs = dscr("QPs", [128, 16, TOWN], BF16)
    WGb = dscr("WGb", [D, 2 * D], BF16)
    WAb = dscr("WAb", [512, D], BF16)
    WBb = dscr("WBb", [1024, D], BF16)
    WOb = dscr("WOb", [D, D], BF16)
    WQb = dscr("WQb", [D, 2048], BF16)
    WPGb = dscr("WPGb", [D, D], BF16)
    WPPb = dscr("WPPb", [PLE, D], BF16)

    S = Sched()
    A = S.add

    with ExitStack() as es:
        def sb_raw(name, shape, dt):
            return es.enter_context(nc.sbuf_tensor(name, list(shape), dt))

        ARENA_F32 = 198 * 256
        arena_t = sb_raw("arena", [128, ARENA_F32], F32)
        aoff = [0]

        def areset():
            S.fence()
            aoff[0] = 0

        def sb(name, shape, dt):
            n = 1
            for d_ in shape[1:]:
                n *= d_
            esz = 4 if dt in (F32, U32) else 2
            nf = (n * esz + 3) // 4
            assert aoff[0] + nf <= ARENA_F32, ("arena overflow", name, aoff[0], nf)
            v = arena_t[:, aoff[0]:aoff[0] + nf]
            aoff[0] += nf
            if dt != F32:
                v = v.bitcast(dt)
            v = v[:, 0:n]
            if len(shape) == 3:
                v = v.rearrange("p (a b) -> p a b", a=shape[1])
            elif len(shape) == 4:
                v = v.rearrange("p (a b c) -> p a b c", a=shape[1], b=shape[2])
            return v

        PS = [es.enter_context(nc.psum_tensor("ps%d" % i, [128, 512], F32)) for i in range(8)]
        PK = ["ps%d" % i for i in range(8)]

        identf = sb_raw("identf", [128, 128], F32)
        cbf = sb_raw("cbf", [128, 3, 128], BF16)
        maskA = sb_raw("maskA", [128, 3, 128], BF16)
        maskB = sb_raw("maskB", [128, 3, 512], BF16)
        iota16 = sb_raw("iota16", [128, 2, 16], F32)
        thr16 = iota16[:, 1, :]
        esk = sb_raw("esk", [128, 8], F32)
        esinkB = sb_raw("esinkB", [128, 2, 4, 128], F32)
        A("sp", lambda h: h.dma_start(out=identf[:], in_=c_identf[:, :]), writes=["identf"], dma=True)
        A("sp", lambda h: h.dma_start(out=cbf[:], in_=c_bf[:, :, :]), writes=["cbf"], dma=True)
        A("sp", lambda h: h.dma_start(out=maskA[:], in_=c_maskA[:, :, :]), writes=["maskA"], dma=True)
        A("sp", lambda h: h.dma_start(out=maskB[:], in_=c_maskB[:, :, :]), writes=["maskB"], dma=True)
        A("sp", lambda h: h.dma_start(out=iota16[:], in_=c_iota[:, :, :]), writes=["iota16"], dma=True)
        A("sp", lambda h: h.dma_start(out=esk[:], in_=sinks.partition_broadcast(128)), writes=["esk"], dma=True)
        A("act", lambda h: h.activation(out=esk[:], in_=esk[:], func=AF.Exp), reads=["esk"], writes=["esk"])
        for kv in range(2):
            A("dve", lambda h, kv=kv: h.tensor_copy(
                out=esinkB[:, kv, :, :], in_=esk[:, 4 * kv:4 * kv + 4].unsqueeze(2).broadcast_to([128, 4, 128])),
              reads=["esk"], writes=["esinkB"])
        identb = cbf[:, 0, :]
        rotp = cbf[:, 1, :]
        onesb = cbf[:, 2, :]

        cs = sb("cs", [128, 2, CTX], BF16)
        A("sp", lambda h: h.dma_start(out=cs[:], in_=c_cs[:, :, :]), writes=["cs"], dma=True)
        xin, xT, wbig = [], [], []

        def alloc_shared(n_xin, ts):
            xin[:] = [sb("xin%d" % i, [128, D], F32) for i in range(n_xin)]
            xT[:] = [sb("xT%d" % i, [128, KC, ts], BF16) for i in range(2)]
            wbig[:] = [sb("wbig%d" % i, [128, KC, DG], BF16) for i in range(2)]

        alloc_shared(4, TS)
        wcnt = [0]
        CAST_ROWS = 256
        cast_pending = [(t_, c_) for c_ in range(N_EXP // CAST_ROWS) for t_ in range(2)]
        uvb_keys = [("UVB", t_, c_) for (t_, c_) in cast_pending]
        tcf, tcb = [], []
        tc_cnt = [0]
        tc_store = []

        def flush_cast_store():
            while tc_store:
                tb_, key_, r0, t_, kk = tc_store.pop(0)
                A("sp", lambda h, tb_=tb_, r0=r0, t_=t_: h.dma_start(
                    out=UVB[r0:r0 + CAST_ROWS, t_, :].rearrange("(p a) d -> p a d", a=2), in_=tb_[:]),
                  reads=[key_], writes=[kk], dma=True)

        def emit_cast(n):
            for _ in range(n):
                if not cast_pending:
                    flush_cast_store()
                    return
                t_, c_ = cast_pending.pop(0)
                i = tc_cnt[0] % 2
                tc_cnt[0] += 1
                srct = peer_u if t_ == 0 else peer_v
                r0 = c_ * CAST_ROWS
                tf_, tb_ = tcf[i], tcb[i]
                A("sp", lambda h, tf_=tf_, srct=srct, r0=r0: h.dma_start(
                    out=tf_[:], in_=srct[r0:r0 + CAST_ROWS, :].rearrange("(p a) d -> p a d", a=2)),
                  writes=["tcf%d" % i], dma=True)
                flush_cast_store()
                A("act", lambda h, tf_=tf_, tb_=tb_: h.activation(out=tb_[:], in_=tf_[:], func=AF.Copy),
                  reads=["tcf%d" % i], writes=["tcb%d" % i])
                tc_store.append((tb_, "tcb%d" % i, r0, t_, ("UVB", t_, c_)))

        def load_w(src_ap, rows_kc, cols, key_extra=None):
            i = wcnt[0] % 2
            wcnt[0] += 1
            buf = wbig[i]
            view = buf[:, 0:rows_kc, 0:cols]
            A("pool", lambda h: h.dma_start(out=view, in_=src_ap.rearrange("(kc p) n -> p kc n", p=128)),
              writes=["wbig%d" % i], dma=True)
            return view, "wbig%d" % i

        def load_wb(src_ap, rows_kc, cols, name):
            i = wcnt[0] % 2
            wcnt[0] += 1
            view = wbig[i][:, 0:rows_kc, 0:cols]
            kl = ["wbig%d" % i]
            A("sp", lambda h: h.dma_start(out=view, in_=src_ap.rearrange("(kc p) n -> p kc n", p=128)),
              reads=wb_keys[name], writes=[kl], dma=True)
            return view, kl

        tcnt = [0]

        def load_transpose(src_rows_ap_fn, nsub, xTbuf, xTkey, keep=None, ncols=D, xin_list=None, xkeys=None):
            kcn = ncols // 128
            for sub in range(nsub):
                if xin_list is None:
                    xi = xin[sub % len(xin)]
                    xk = "xin%d" % (sub % len(xin))
                else:
                    xi = xin_list[sub]
                    xk = xkeys[sub]
                if src_rows_ap_fn is not None:
                    A("sp", lambda h, xi=xi, sub=sub: h.dma_start(out=xi[:, 0:ncols], in_=src_rows_ap_fn(sub)),
                      writes=[xk], dma=True)
                for k0 in range(0, kcn, 4):
                    n4 = min(4, kcn - k0)
                    bi = tcnt[0] % 2
                    tcnt[0] += 1
                    for j in range(n4):
                        A("pe", lambda h, xi=xi, bi=bi, j=j, k0=k0: h.transpose(
                            out=PS[bi][:, j * 128:(j + 1) * 128], in_=xi[:, (k0 + j) * 128:(k0 + j + 1) * 128],
                            identity=identf[:]), reads=[xk, "identf"], writes=[PK[bi]])
                    A("act", lambda h, bi=bi, n4=n4, k0=k0, sub=sub: h.activation(
                        out=xTbuf[:, k0:k0 + n4, sub * 128:(sub + 1) * 128],
                        in_=PS[bi][:, 0:n4 * 128].rearrange("p (a b) -> p a b", a=n4), func=AF.Copy),
                      reads=[PK[bi]], writes=[xTkey])

        qraw = [sb("qraw%d" % i, [128, TS], BF16) for i in range(2)]
        t1 = [sb("t1_%d" % i, [128, TS], F32) for i in range(2)]
        t2 = [sb("t2_%d" % i, [128, TS], F32) for i in range(2)]
        qout = [sb("qout%d" % i, [128, TS], BF16) for i in range(3)]
        vout = [sb("vout%d" % i, [128, 512], BF16) for i in range(3)]
        ccnt = [0]
        vcnt = [0]
        qk_written = {}
        v_written = {}

        NT_H = HALO // TS
        for ti in range(NT_H + TOWN // TS):
            halo = ti < NT_H
            src = xh if halo else xo
            row0 = (ti % NT_H) * TS if halo else (ti - NT_H) * TS
            ctx0 = ti * TS
            xTb = xT[ti % 2]
            xTk = "xT%d" % (ti % 2)
            load_transpose(lambda sub, src=src, row0=row0: src[row0 + sub * 128: row0 + (sub + 1) * 128, :], 4, xTb, xTk)
            if halo and ti < NT_H - 1:
                groups = [5, 8]
            elif halo:
                groups = [3, 4, 5, 6, 7, 8, 11]
            else:
                groups = list(range(12))
            for gi in groups:
                wv, wk = load_w(w_in[:, gi * 512:(gi + 1) * 512], KC, 512)
                chunks = []
                vparts = []
                if gi < 3:
                    chunks = [("q", gi * 4 + c, c) for c in range(4)]
                elif gi < 6:
                    chunks = [("k", (gi - 3) * 4 + c, c) for c in range(4)]
                elif gi < 9:
                    vparts = [(0, 512, (gi - 6) * 4, 4)]
                elif gi < 11:
                    chunks = [("q", 12 + (gi - 9) * 4 + c, c) for c in range(4)]
                else:
                    chunks = [("k", 12, 0), ("k", 13, 1)]
                    vparts = [(256, 256, 12, 2)]
                for (typ, head, c) in chunks:
                    i = ccnt[0]
                    ccnt[0] += 1
                    pb = 2 + (i % 3)
                    pw = 5 + (i % 2)
                    for kc in range(KC):
                        A("pe", lambda h, pb=pb, kc=kc, c=c, wv=wv, xTb=xTb: h.matmul(
                            PS[pb][:, :], lhsT=wv[:, kc, c * 128:(c + 1) * 128], rhs=xTb[:, kc, :],
                            start=(kc == 0), stop=(kc == KC - 1)), reads=[wk, xTk], writes=[PK[pb]])
                    qr = qraw[i % 2]
                    qrk = "qraw%d" % (i % 2)
                    A("act", lambda h, pb=pb, qr=qr: h.activation(out=qr[:], in_=PS[pb][:, :], func=AF.Copy),
                      reads=[PK[pb]], writes=[qrk])
                    A("pe", lambda h, pw=pw, qr=qr: h.matmul(PS[pw][:, :], lhsT=rotp, rhs=qr[:], start=True, stop=True),
                      reads=[qrk, "cbf"], writes=[PK[pw]])
                    tt1, tt2 = t1[i % 2], t2[i % 2]
                    A("dve", lambda h, tt1=tt1, qr=qr, ctx0=ctx0: h.tensor_tensor(
                        out=tt1[:], in0=qr[:], in1=cs[:, 0, ctx0:ctx0 + TS], op=ALU.mult),
                      reads=[qrk, "cs"], writes=["t1_%d" % (i % 2)])
                    A("dve", lambda h, tt2=tt2, pw=pw, ctx0=ctx0: h.tensor_tensor(
                        out=tt2[:], in0=PS[pw][:, :], in1=cs[:, 1, ctx0:ctx0 + TS], op=ALU.mult),
                      reads=[PK[pw], "cs"], writes=["t2_%d" % (i % 2)])
                    qo = qout[i % 3]
                    qok = "qout%d" % (i % 3)
                    A("dve", lambda h, qo=qo, tt1=tt1, tt2=tt2: h.tensor_tensor(out=qo[:], in0=tt1[:], in1=tt2[:], op=ALU.add),
                      reads=["t1_%d" % (i % 2), "t2_%d" % (i % 2)], writes=[qok])
                    if typ == "q":
                        dst = QT[head, :, ctx0 - HALO:ctx0 - HALO + TS]
                        key = ("QT", head, ti)
                        qk_written.setdefault(("QT", head), []).append(key)
                    else:
                        dst = KT[head, :, ctx0:ctx0 + TS]
                        key = ("KT", head, ti)
                        qk_written.setdefault(("KT", head), []).append(key)
                    A("sp", lambda h, dst=dst, qo=qo: h.dma_start(out=dst, in_=qo[:]), reads=[qok], writes=[key], dma=True)
                for (c0, ncol, head0, nh) in vparts:
                    for sub in range(4):
                        i = vcnt[0]
                        vcnt[0] += 1
                        pb = 2 + (ccnt[0] + i) % 3
                        ccnt[0] += 0
                        pb = 7
                        for kc in range(KC):
                            A("pe", lambda h, pb=pb, kc=kc, sub=sub, wv=wv, xTb=xTb, c0=c0, ncol=ncol: h.matmul(
                                PS[pb][:, 0:ncol], lhsT=xTb[:, kc, sub * 128:(sub + 1) * 128], rhs=wv[:, kc, c0:c0 + ncol],
                                start=(kc == 0), stop=(kc == KC - 1)), reads=[wk, xTk], writes=[PK[pb]])
                        vo = vout[i % 3]
                        vok = "vout%d" % (i % 3)
                        A("act", lambda h, pb=pb, vo=vo, ncol=ncol: h.activation(out=vo[:, 0:ncol], in_=PS[pb][:, 0:ncol], func=AF.Copy),
                          reads=[PK[pb]], writes=[vok])
                        r0 = ctx0 + sub * 128
                        key = ("VT", head0, ti, sub)
                        for hh in range(nh):
                            v_written.setdefault(("VT", head0 + hh), []).append(key)
                        A("sp", lambda h, vo=vo, r0=r0, head0=head0, nh=nh, ncol=ncol: h.dma_start(
                            out=VT[r0:r0 + 128, head0:head0 + nh, :],
                            in_=vo[:, 0:ncol].rearrange("p (a b) -> p a b", a=nh)), reads=[vok], writes=[key], dma=True)

        wb_keys = {}

        def precast(name, dst, src, rows, cols):
            keys = []
            for c0 in range(0, cols, 2048):
                cw = min(2048, cols - c0)
                for r0 in range(0, rows, 1024):
                    rw = min(1024, rows - r0)
                    k_ = ("WB", name, c0, r0)
                    keys.append(k_)
                    A("pool", lambda h, c0=c0, cw=cw, r0=r0, rw=rw: h.dma_start(out=dst[r0:r0 + rw, c0:c0 + cw], in_=src[r0:r0 + rw, c0:c0 + cw]),
                      writes=[k_], dma=True)
            wb_keys[name] = keys

        precast("WG", WGb, w_in[:, 6144:6144 + 2 * D], D, 2 * D)
        precast("WA", WAb, w_ba, 512, D)
        precast("WB", WBb, w_bb, 1024, D)
        precast("WO", WOb, w_out, D, D)
        precast("WQ", WQb, peer_wq, D, 2048)
        precast("WPG", WPGb, ple_gate, D, D)
        precast("WPP", WPPb, ple_proj, PLE, D)
        areset()
        qt = [sb("qt%d" % i, [128, TOWN], BF16) for i in range(2)]
        kt = [sb("kt%d" % i, [128, CTX], BF16) for i in range(2)]
        vp = [sb("vp%d" % i, [128, CTX], BF16) for i in range(2)]
        acc = sb("acc", [128, 2, TOWN], F32)
        rden = sb("rden", [128, TOWN], F32)
        oas = sb("oas", [128, TOWN], BF16)
        pT = [sb("pT%d" % i, [128, 256], BF16) for i in range(2)]
        hcnt = 0
        bcnt = 0
        for slot in range(4):
            for g in range(3):
                head = g * 4 + slot
                dil = A_PATTERNS[g][1]
                halo_g = 128 * dil
                nblk = TOWN // (128 * dil)
                hb = hcnt % 2
                hcnt += 1
                qtb, ktb, vpb = qt[hb], kt[hb], vp[hb]
                qk_, kk_, vk_ = "qt%d" % hb, "kt%d" % hb, "vp%d" % hb
                clen = halo_g + TOWN
                A("sp", lambda h, qtb=qtb, head=head: h.dma_start(out=qtb[:], in_=QT[head, :, :]),
                  reads=qk_written[("QT", head)], writes=[qk_], dma=True)
                A("sp", lambda h, ktb=ktb, head=head, clen=clen: h.dma_start(out=ktb[:, 0:clen], in_=KT[head, :, CTX - clen:CTX]),
                  reads=qk_written[("KT", head)], writes=[kk_], dma=True)
                vview = vpb[:, 0:clen].rearrange("p (b r d) -> p b r d", r=dil, d=128) if False else None
                nB = nblk + 1
                vdst = vpb[:, 0:nB * dil * 128].rearrange("p (b r d) -> p b r d", b=nB, r=dil)
                vsrc = VT[CTX - clen:CTX, head, :].rearrange("(b i r) d -> i b r d", b=nB, i=128, r=dil)
                for B_ in range(nB):
                    A("sp", lambda h, vdst=vdst, vsrc=vsrc, B_=B_: h.dma_start(out=vdst[:, B_, :, :], in_=vsrc[:, B_, :, :]),
                      reads=v_written[("VT", head)], writes=[vk_], dma=True)
                qv = qtb[:].rearrange("p (n r) -> p r n", r=dil)
                kv_ = ktb[:, 0:clen].rearrange("p (n r) -> p r n", r=dil)
                accv = acc[:].rearrange("p a (n r) -> p a r n", r=dil)
                for r in range(dil):
                    for b in range(nblk):
                        i = bcnt
                        bcnt += 1
                        sbk = i % 2
                        obk = 2 + i % 2
                        qs = qv[:, r, b * 128:(b + 1) * 128]
                        kp = kv_[:, r, b * 128:(b + 1) * 128]
                        kc_ = kv_[:, r, (b + 1) * 128:(b + 2) * 128]
                        mprev = maskA[:, 2, :] if b == 0 else maskA[:, 0, :]
                        A("pe", lambda h, sbk=sbk, kp=kp, qs=qs: h.matmul(PS[sbk][:, 0:128], lhsT=kp, rhs=qs, start=True, stop=False),
                          reads=[qk_, kk_], writes=[PK[sbk]])
                        A("pe", lambda h, sbk=sbk, mprev=mprev: h.matmul(PS[sbk][:, 0:128], lhsT=identb, rhs=mprev, start=False, stop=True),
                          reads=["cbf", "maskA"], writes=[PK[sbk]])
                        A("pe", lambda h, sbk=sbk, kc_=kc_, qs=qs: h.matmul(PS[sbk][:, 128:256], lhsT=kc_, rhs=qs, start=True, stop=False),
                          reads=[qk_, kk_], writes=[PK[sbk]])
                        A("pe", lambda h, sbk=sbk: h.matmul(PS[sbk][:, 128:256], lhsT=identb, rhs=maskA[:, 1, :], start=False, stop=True),
                          reads=["cbf", "maskA"], writes=[PK[sbk]])
                        pt = pT[i % 2]
                        ptk = "pT%d" % (i % 2)
                        A("act", lambda h, pt=pt, sbk=sbk: h.activation(out=pt[:], in_=PS[sbk][:, 0:256], func=AF.Exp, scale=scale),
                          reads=[PK[sbk]], writes=[ptk])
                        A("pe", lambda h, obk=obk, pt=pt, vdst=vdst, b=b, r=r: h.matmul(
                            PS[obk][:, 0:128], lhsT=vdst[:, b, r, :], rhs=pt[:, 0:128], start=True, stop=False),
                          reads=[ptk, vk_], writes=[PK[obk]])
                        A("pe", lambda h, obk=obk, pt=pt, vdst=vdst, b=b, r=r: h.matmul(
                            PS[obk][:, 0:128], lhsT=vdst[:, b + 1, r, :], rhs=pt[:, 128:256], start=False, stop=True),
                          reads=[ptk, vk_], writes=[PK[obk]])
                        A("pe", lambda h, obk=obk, pt=pt: h.matmul(PS[obk][:, 128:256], lhsT=onesb, rhs=pt[:, 0:128], start=True, stop=False),
                          reads=[ptk, "cbf"], writes=[PK[obk]])
                        A("pe", lambda h, obk=obk, pt=pt: h.matmul(PS[obk][:, 128:256], lhsT=onesb, rhs=pt[:, 128:256], start=False, stop=True),
                          reads=[ptk, "cbf"], writes=[PK[obk]])
                        av = accv[:, :, r, b * 128:(b + 1) * 128]
                        ov = PS[obk][:, 0:256].rearrange("p (a q) -> p a q", a=2)
                        akey = "acc"
                        if g == 0:
                            A("dve", lambda h, av=av, ov=ov: h.tensor_copy(out=av, in_=ov), reads=[PK[obk]], writes=[akey])
                        else:
                            A("dve", lambda h, av=av, ov=ov: h.tensor_tensor(out=av, in0=av, in1=ov, op=ALU.add),
                              reads=[PK[obk], akey], writes=[akey])
            akeys = ["acc"]
            A("dve", lambda h: h.reciprocal(out=rden[:], in_=acc[:, 1, :]), reads=akeys, writes=["rden"])
            A("dve", lambda h: h.tensor_tensor(out=oas[:], in0=acc[:, 0, :], in1=rden[:], op=ALU.mult),
              reads=akeys + ["rden"], writes=["oas"])
            A("sp", lambda h, slot=slot: h.dma_start(out=OT[slot, :, :], in_=oas[:]), reads=["oas"], writes=[("OT", slot)], dma=True)

        qtB = sb("qtB", [128, 4, TOWN], BF16)
        obuf = sb("obuf", [128, 4, TOWN], BF16)
        pTB = [sb("pTB%d" % i, [128, 2, 512], BF16) for i in range(2)]
        denB = sb("denB", [128, 512], F32)
        for kvh in range(2):
            hb = hcnt % 2
            hcnt += 1
            ktb, vpb = kt[hb], vp[hb]
            kk_, vk_ = "kt%d" % hb, "vp%d" % hb
            clen = 128 + TOWN
            nB = TOWN // 128 + 1
            for hh in range(4):
                A("sp", lambda h, hh=hh, kvh=kvh: h.dma_start(out=qtB[:, hh, :], in_=QT[12 + 4 * kvh + hh, :, :]),
                  reads=qk_written[("QT", 12 + 4 * kvh + hh)], writes=["qtB"], dma=True)
            A("sp", lambda h, ktb=ktb, kvh=kvh, clen=clen: h.dma_start(out=ktb[:, 0:clen], in_=KT[12 + kvh, :, CTX - clen:CTX]),
              reads=qk_written[("KT", 12 + kvh)], writes=[kk_], dma=True)
            vdst = vpb[:, 0:nB * 128].rearrange("p (b d) -> p b d", b=nB)
            vsrc = VT[CTX - clen:CTX, 12 + kvh, :].rearrange("(b i) d -> i b d", i=128)
            A("sp", lambda h, vdst=vdst, vsrc=vsrc: h.dma_start(out=vdst, in_=vsrc),
              reads=v_written[("VT", 12 + kvh)], writes=[vk_], dma=True)
            for b in range(TOWN // 128):
                j = b % 2
                b0, b1, b2, b3 = 4 * j, 4 * j + 1, 4 * j + 2, 4 * j + 3
                qs = qtB[:, :, b * 128:(b + 1) * 128]
                mprev = maskB[:, 2, :] if b == 0 else maskB[:, 0, :]
                A("pe", lambda h, b0=b0, b=b, qs=qs, ktb=ktb: h.matmul(PS[b0][:, :], lhsT=ktb[:, b * 128:(b + 1) * 128], rhs=qs, start=True, stop=False),
                  reads=["qtB", kk_], writes=[PK[b0]])
                A("pe", lambda h, b0=b0, mprev=mprev: h.matmul(PS[b0][:, :], lhsT=identb, rhs=mprev, start=False, stop=True),
                  reads=["cbf", "maskB"], writes=[PK[b0]])
                A("pe", lambda h, b1=b1, b=b, qs=qs, ktb=ktb: h.matmul(PS[b1][:, :], lhsT=ktb[:, (b + 1) * 128:(b + 2) * 128], rhs=qs, start=True, stop=False),
                  reads=["qtB", kk_], writes=[PK[b1]])
                A("pe", lambda h, b1=b1: h.matmul(PS[b1][:, :], lhsT=identb, rhs=maskB[:, 1, :], start=False, stop=True),
                  reads=["cbf", "maskB"], writes=[PK[b1]])
                pt = pTB[j]
                ptk = "pTB%d" % j
                A("act", lambda h, pt=pt, b0=b0: h.activation(out=pt[:, 0, :], in_=PS[b0][:, :], func=AF.Exp, scale=scale),
                  reads=[PK[b0]], writes=[ptk + "a"])
                A("act", lambda h, pt=pt, b1=b1: h.activation(out=pt[:, 1, :], in_=PS[b1][:, :], func=AF.Exp, scale=scale),
                  reads=[PK[b1]], writes=[ptk + "b"])
                A("pe", lambda h, b2=b2, pt=pt, vdst=vdst, b=b: h.matmul(PS[b2][:, :], lhsT=vdst[:, b, :], rhs=pt[:, 0, :], start=True, stop=False),
                  reads=[ptk + "a", vk_], writes=[PK[b2]])
                A("pe", lambda h, b2=b2, pt=pt, vdst=vdst, b=b: h.matmul(PS[b2][:, :], lhsT=vdst[:, b + 1, :], rhs=pt[:, 1, :], start=False, stop=True),
                  reads=[ptk + "b", vk_], writes=[PK[b2]])
                A("pe", lambda h, b3=b3, pt=pt: h.matmul(PS[b3][:, :], lhsT=onesb, rhs=pt[:, 0, :], start=True, stop=False),
                  reads=[ptk + "a", "cbf"], writes=[PK[b3]])
                A("pe", lambda h, b3=b3, pt=pt: h.matmul(PS[b3][:, :], lhsT=onesb, rhs=pt[:, 1, :], start=False, stop=True),
                  reads=[ptk + "b", "cbf"], writes=[PK[b3]])
                A("dve", lambda h, b3=b3, kvh=kvh: h.tensor_tensor(
                    out=denB[:], in0=PS[b3][:, :], in1=esinkB[:, kvh, :, :].rearrange("p a b -> p (a b)"), op=ALU.add),
                  reads=[PK[b3], "esinkB"], writes=["denB"])
                A("dve", lambda h: h.reciprocal(out=denB[:], in_=denB[:]), reads=["denB"], writes=["denB"])
                A("dve", lambda h, b2=b2, b=b: h.tensor_tensor(
                    out=obuf[:, :, b * 128:(b + 1) * 128], in0=PS[b2][:, :].rearrange("p (a q) -> p a q", a=4),
                    in1=denB[:].rearrange("p (a q) -> p a q", a=4), op=ALU.mult),
                  reads=[PK[b2], "denB"], writes=["obuf"])
            for hh in range(4):
                A("sp", lambda h, hh=hh, kvh=kvh: h.dma_start(out=OT[4 + 4 * kvh + hh, :, :], in_=obuf[:, hh, :]),
                  reads=["obuf"], writes=[("OT", 4 + 4 * kvh + hh)], dma=True)

        areset()
        alloc_shared(NSUB, TS2)
        oat = sb("oat", [128, 12, TS2], BF16)
        mq = sb("mq", [128, max(KC, 16), TS2], BF16)
        mergedT = mq[:, 0:KC, :]
        qpT = mq[:, 0:16, :]
        wsm_a = sb("wsm_a", [128, 4, DG], BF16)
        wsm_b = sb("wsm_b", [128, 8, DG], BF16)
        wpp = sb("wpp", [128, 2, D], BF16)
        ppT = sb("ppT", [128, 2, TS2], BF16)
        sg = [sb("sg%d" % i, [128, 512], F32) for i in range(2)]
        mt = [sb("mt%d" % i, [128, 512], F32) for i in range(2)]
        lngA = sb("lngA", [128, D], F32)
        lnbA = sb("lnbA", [128, D], F32)
        junkA = sb("junkA", [128, D], BF16)
        st1A = sb("st1A", [128, 8], F32)
        epstA = sb("epstA", [128, 8], F32)
        pstg = sb("pstg", [128, NSUB * PLE], F32)
        tcf[:] = [sb("tcf%d" % i, [128, 2, D], F32) for i in range(2)]
        tcb[:] = [sb("tcb%d" % i, [128, 2, D], BF16) for i in range(2)]
        A("dve", lambda h: h.memset(epstA[:], LN_EPS), writes=["epstA"])
        A("sp", lambda h: h.dma_start(out=wpp[:], in_=WPPb.rearrange("(kc p) n -> p kc n", p=128)), reads=wb_keys["WPP"], writes=["wpp"], dma=True)
        A("sp", lambda h: h.dma_start(out=lngA[:], in_=ln1_g.partition_broadcast(128)), writes=["lngA"], dma=True)
        A("sp", lambda h: h.dma_start(out=lnbA[:], in_=ln1_b.partition_broadcast(128)), writes=["lnbA"], dma=True)

        gcnt = [0]
        out_ops = []

        def layer_norm(xi, xk, lng, lnb, junk, st1, epst, tg):
            A("dve", lambda h: h.reduce_sum(out=st1[:, 0:1], in_=xi[:], axis=AX.X), reads=[xk], writes=["st1a" + tg])
            A("dve", lambda h: h.tensor_scalar(out=st1[:, 1:2], in0=st1[:, 0:1], scalar1=-1.0 / D, scalar2=None, op0=ALU.mult),
              reads=["st1a" + tg], writes=["st1b" + tg])
            A("act", lambda h: h.activation(out=junk[:], in_=xi[:], func=AF.Square, bias=st1[:, 1:2], accum_out=st1[:, 2:3]),
              reads=[xk, "st1b" + tg], writes=["junk" + tg, "st1c" + tg])
            A("act", lambda h: h.activation(out=st1[:, 3:4], in_=st1[:, 2:3], func=AF.Sqrt, scale=1.0 / D, bias=epst[:, 0:1]),
              reads=["st1c" + tg, "epst" + tg], writes=["st1d" + tg])
            A("dve", lambda h: h.reciprocal(out=st1[:, 4:5], in_=st1[:, 3:4]), reads=["st1d" + tg], writes=["st1e" + tg])
            A("dve", lambda h: h.tensor_scalar(out=xi[:], in0=xi[:], scalar1=st1[:, 1:2], scalar2=st1[:, 4:5], op0=ALU.add, op1=ALU.mult),
              reads=[xk, "st1b" + tg, "st1e" + tg], writes=[xk])
            A("dve", lambda h: h.tensor_tensor(out=xi[:], in0=xi[:], in1=lng[:], op=ALU.mult), reads=[xk, "lng" + tg], writes=[xk])
            A("dve", lambda h: h.tensor_tensor(out=xi[:], in0=xi[:], in1=lnb[:], op=ALU.add), reads=[xk, "lnb" + tg], writes=[xk])

        for st in range(NSUP):
            T0 = st * TS2
            h1src = [xin[i_][:] for i_ in range(NSUB)]
            xTb = xT[st % 2]
            xTk = "xT%d" % (st % 2)
            xTb2 = xT[(st + 1) % 2]
            xTk2 = "xT%d" % ((st + 1) % 2)
            load_transpose(lambda sub, T0=T0: xo[T0 + sub * 128:T0 + (sub + 1) * 128, :], NSUB, xTb, xTk)
            A("sp", lambda h, T0=T0: h.dma_start(out=oat[:], in_=OT[:, :, T0:T0 + TS2].rearrange("a p t -> p a t")),
              reads=[("OT", j) for j in range(12)], writes=["oat"], dma=True)
            for dg in range(NDG):
                wga, wgak = load_wb(WGb[:, dg * DG:(dg + 1) * DG], KC, DG, "WG")
                wgb, wgbk = load_wb(WGb[:, D + dg * DG:D + (dg + 1) * DG], KC, DG, "WG")
                A("sp", lambda h, dg=dg: h.dma_start(out=wsm_a[:], in_=WAb[:, dg * DG:(dg + 1) * DG].rearrange("(kc p) n -> p kc n", p=128)),
                  reads=wb_keys["WA"], writes=["wsm_a"], dma=True)
                A("sp", lambda h, dg=dg: h.dma_start(out=wsm_b[:], in_=WBb[:, dg * DG:(dg + 1) * DG].rearrange("(kc p) n -> p kc n", p=128)),
                  reads=wb_keys["WB"], writes=["wsm_b"], dma=True)
                for c in range(DG // 128):
                    dc = dg * (DG // 128) + c
                    cs_ = slice(c * 128, (c + 1) * 128)
                    for kc in range(KC):
                        A("pe", lambda h, kc=kc, cs_=cs_, wga=wga, xTb=xTb: h.matmul(PS[2][:, 0:TS2], lhsT=wga[:, kc, cs_], rhs=xTb[:, kc, :], start=(kc == 0), stop=(kc == KC - 1)),
                          reads=[wgak, xTk], writes=[PK[2]])
                    for kc in range(KC):
                        A("pe", lambda h, kc=kc, cs_=cs_, wgb=wgb, xTb=xTb: h.matmul(PS[3][:, 0:TS2], lhsT=wgb[:, kc, cs_], rhs=xTb[:, kc, :], start=(kc == 0), stop=(kc == KC - 1)),
                          reads=[wgbk, xTk], writes=[PK[3]])
                    for kc in range(4):
                        A("pe", lambda h, kc=kc, cs_=cs_: h.matmul(PS[4][:, 0:TS2], lhsT=wsm_a[:, kc, cs_], rhs=oat[:, kc, :], start=(kc == 0), stop=(kc == 3)),
                          reads=["wsm_a", "oat"], writes=[PK[4]])
                    for kc in range(8):
                        A("pe", lambda h, kc=kc, cs_=cs_: h.matmul(PS[5][:, 0:TS2], lhsT=wsm_b[:, kc, cs_], rhs=oat[:, 4 + kc, :], start=(kc == 0), stop=(kc == 7)),
                          reads=["wsm_b", "oat"], writes=[PK[5]])
                    A("act", lambda h: h.activation(out=sg[0][:, 0:TS2], in_=PS[2][:, 0:TS2], func=AF.Sigmoid), reads=[PK[2]], writes=["sg0"])
                    A("act", lambda h: h.activation(out=sg[1][:, 0:TS2], in_=PS[3][:, 0:TS2], func=AF.Sigmoid), reads=[PK[3]], writes=["sg1"])
                    A("dve", lambda h: h.tensor_tensor(out=mt[0][:, 0:TS2], in0=sg[0][:, 0:TS2], in1=PS[4][:, 0:TS2], op=ALU.mult), reads=["sg0", PK[4]], writes=["mt0"])
                    A("dve", lambda h: h.tensor_tensor(out=mt[1][:, 0:TS2], in0=sg[1][:, 0:TS2], in1=PS[5][:, 0:TS2], op=ALU.mult), reads=["sg1", PK[5]], writes=["mt1"])
                    A("dve", lambda h, dc=dc: h.tensor_tensor(out=mergedT[:, dc, :], in0=mt[0][:, 0:TS2], in1=mt[1][:, 0:TS2], op=ALU.add),
                      reads=["mt0", "mt1"], writes=["mq"])
            for dg in range(NDG):
                wo, wok = load_wb(WOb[:, dg * DG:(dg + 1) * DG], KC, DG, "WO")
                for sub in range(NSUB):
                    pb = 6 + sub % 2
                    for kc in range(KC):
                        A("pe", lambda h, pb=pb, kc=kc, sub=sub, wo=wo: h.matmul(PS[pb][:, 0:DG], lhsT=mergedT[:, kc, sub * 128:(sub + 1) * 128], rhs=wo[:, kc, :],
                                                                         start=(kc == 0), stop=(kc == KC - 1)), reads=["mq", wok], writes=[PK[pb]])
                    xi = xin[sub]
                    xk = "xin%d" % sub
                    A("dve", lambda h, pb=pb, xi=xi, dg=dg: h.scalar_tensor_tensor(
                        out=xi[:, dg * DG:(dg + 1) * DG], in0=xi[:, dg * DG:(dg + 1) * DG], scalar=alpha, in1=PS[pb][:, 0:DG],
                        op0=ALU.mult, op1=ALU.add), reads=[PK[pb], xk], writes=[xk])
            for sub in range(NSUB):
                layer_norm(xin[sub], "xin%d" % sub, lngA, lnbA, junkA, st1A, epstA, "A")
            load_transpose(None, NSUB, xTb2, xTk2, xin_list=xin, xkeys=["xin%d" % i for i in range(NSUB)])
            for g4 in range(4):
                wqv, wqk = load_wb(WQb[:, g4 * 512:(g4 + 1) * 512], KC, 512, "WQ")
                for c in range(4):
                    hc = g4 * 4 + c
                    pb = 2 + hc % 2
                    for kc in range(KC):
                        A("pe", lambda h, pb=pb, kc=kc, c=c, wqv=wqv, xTb2=xTb2: h.matmul(PS[pb][:, 0:TS2], lhsT=wqv[:, kc, c * 128:(c + 1) * 128], rhs=xTb2[:, kc, :],
                                                                           start=(kc == 0), stop=(kc == KC - 1)), reads=[wqk, xTk2], writes=[PK[pb]])
                    A("act", lambda h, pb=pb, hc=hc: h.activation(out=qpT[:, hc, :], in_=PS[pb][:, 0:TS2], func=AF.Copy), reads=[PK[pb]], writes=["mq"])
            for sub in range(NSUB):
                gb_ = pstg[:, sub * PLE:(sub + 1) * PLE]
                gk_ = "pstg"
                A("sp", lambda h, gb_=gb_, sub=sub, T0=T0: h.dma_start(out=gb_, in_=pin[T0 + sub * 128:T0 + (sub + 1) * 128, :]), writes=[gk_], dma=True)
                for j in range(2):
                    A("pe", lambda h, gb_=gb_, j=j: h.transpose(out=PS[4][:, j * 128:(j + 1) * 128], in_=gb_[:, j * 128:(j + 1) * 128], identity=identf[:]),
                      reads=[gk_, "identf"], writes=[PK[4]])
                A("act", lambda h, sub=sub: h.activation(out=ppT[:, :, sub * 128:(sub + 1) * 128], in_=PS[4][:, 0:256].rearrange("p (a b) -> p a b", a=2), func=AF.Copy),
                  reads=[PK[4]], writes=["ppT"])
            for dg in range(NDG):
                wg_, wgk_ = load_wb(WPGb[:, dg * DG:(dg + 1) * DG], KC, DG, "WPG")
                for sub in range(NSUB):
                    pb = 6 + sub % 2
                    pb2 = 4 + sub % 2
                    for kc in range(KC):
                        A("pe", lambda h, pb=pb, kc=kc, sub=sub, wg_=wg_, xTb2=xTb2: h.matmul(PS[pb][:, 0:DG], lhsT=xTb2[:, kc, sub * 128:(sub + 1) * 128], rhs=wg_[:, kc, :],
                                                                           start=(kc == 0), stop=(kc == KC - 1)), reads=[xTk2, wgk_], writes=[PK[pb]])
                    for kc in range(2):
                        A("pe", lambda h, pb2=pb2, kc=kc, sub=sub, dg=dg: h.matmul(PS[pb2][:, 0:DG], lhsT=ppT[:, kc, sub * 128:(sub + 1) * 128], rhs=wpp[:, kc, dg * DG:(dg + 1) * DG],
                                                                            start=(kc == 0), stop=(kc == 1)), reads=["ppT", "wpp"], writes=[PK[pb2]])
                    sgi = sg[sub % 2]
                    sgk = "sg%d" % (sub % 2)
                    mti = mt[sub % 2]
                    mtk = "mt%d" % (sub % 2)
                    A("act", lambda h, pb=pb, sgi=sgi: h.activation(out=sgi[:, 0:DG], in_=PS[pb][:, 0:DG], func=AF.Sigmoid), reads=[PK[pb]], writes=[sgk])
                    A("dve", lambda h, pb2=pb2, sgi=sgi, mti=mti: h.tensor_tensor(out=mti[:, 0:DG], in0=sgi[:, 0:DG], in1=PS[pb2][:, 0:DG], op=ALU.mult),
                      reads=[sgk, PK[pb2]], writes=[mtk])
                    xi = xin[sub]
                    A("dve", lambda h, xi=xi, mti=mti, dg=dg: h.scalar_tensor_tensor(
                        out=mti[:, 0:DG], in0=xi[:, dg * DG:(dg + 1) * DG], scalar=alpha, in1=mti[:, 0:DG], op0=ALU.mult, op1=ALU.add),
                      reads=["xin%d" % sub, mtk], writes=[mtk])
                    A("sp", lambda h, mti=mti, sub=sub, dg=dg, T0=T0: h.dma_start(out=Y2[T0 + sub * 128:T0 + (sub + 1) * 128, dg * DG:(dg + 1) * DG], in_=mti[:, 0:DG]),
                      reads=[mtk], writes=[("Y2", st, sub, dg)], dma=True)
            A("sp", lambda h, T0=T0: h.dma_start(out=QPs[:, :, T0:T0 + TS2], in_=qpT), reads=["mq"], writes=[("QPs", st)], dma=True)
            for sub in range(NSUB):
                A("sp", lambda h, sub=sub, T0=T0, h1src=h1src: h.dma_start(out=H1s[T0 + sub * 128:T0 + (sub + 1) * 128, :], in_=h1src[sub]),
                  reads=["xin%d" % sub], writes=[("H1s", st, sub)], dma=True)
            emit_cast((N_EXP // CAST_ROWS * 2 + NSUP - 1) // NSUP)
        emit_cast(100000)
        flush_cast_store()
        assert not cast_pending
        areset()
        xin[:] = [sb("xinB%d" % i, [128, D], F32) for i in range(NSUB)]
        qpB = sb("qpB", [128, 16, TS2], BF16)
        skT = sb("skT", [128, 16, 128], BF16)
        lngB = sb("lngB", [128, D], F32)
        lnbB = sb("lnbB", [128, D], F32)
        junk = sb("junk", [128, D], BF16)
        junk2 = [junk, None]
        st1B = sb("st1B", [128, 8], F32)
        epstB = sb("epstB", [128, 8], F32)
        A("dve", lambda h: h.memset(epstB[:], LN_EPS), writes=["epstB"])
        s_sb = sb("s_sb", [128, 16, 128], F32)
        s2 = sb("s2", [128, 256], F32)
        m16 = sb("m16", [128, 16, 16], F32)
        i16 = sb("i16", [128, 16, 16], U32)
        i16f = sb("i16f", [128, 16, 16], F32)
        cand = sb("cand", [128, 8, 256], F32)
        junk2[1] = cand[:].rearrange("p a b -> p (a b)").bitcast(BF16)[:, 0:D]
        b16 = sb("b16", [128, 8, 16], F32)
        c16 = sb("c16", [128, 8, 16], U32)
        cfa = sb("cfa", [128, 8, 16], F32)
        cfb = sb("cfb", [128, 8, 16], F32)
        oh = s_sb[:].rearrange("p a b -> p (a b)").rearrange("p (a b c) -> p a b c", a=8, b=16)
        ei = sb("ei", [128, 8, 16], F32)
        ei2 = sb("ei2", [128, 8, 16], F32)
        eidx_l = [sb("eidx%d" % i, [128, 128], U32) for i in range(NSUB)]
        gate_l = [sb("gate%d" % i, [128, 8, 16], F32) for i in range(NSUB)]
        zs = sb("zs", [128, 8], F32)
        actp = sb("actp", [128, 128], F32)
        wgt = sb("wgt", [128, 128], F32)
        NG = 12
        gbuf = [sb("gbuf%d" % i, [128, 2, D], BF16) for i in range(NG)]
        h1b = [sb("h1b%d" % i, [128, D], BF16) for i in range(NSUB)]
        diag = [sb("diag%d" % i, [128, 128], BF16) for i in range(3)]
        ffn = sb("ffn", [128, D], F32)
        A("sp", lambda h: h.dma_start(out=lngB[:], in_=ln2_g.partition_broadcast(128)), writes=["lngB"], dma=True)
        A("sp", lambda h: h.dma_start(out=lnbB[:], in_=ln2_b.partition_broadcast(128)), writes=["lnbB"], dma=True)
        for hc in range(16):
            xi = xin[hc % len(xin)]
            xk = "xin%d" % (hc % len(xin))
            A("sp", lambda h, xi=xi, hc=hc: h.dma_start(out=xi[:, 0:128], in_=peer_sk[hc, :, :]), writes=[xk], dma=True)
            A("pe", lambda h, xi=xi: h.transpose(out=PS[0][:, 0:128], in_=xi[:, 0:128], identity=identf[:]),
              reads=[xk, "identf"], writes=[PK[0]])
            A("act", lambda h, hc=hc: h.activation(out=skT[:, hc, :], in_=PS[0][:, 0:128], func=AF.Copy), reads=[PK[0]], writes=["skT"])

        for st in range(NSUP):
            T0 = st * TS2
            A("sp", lambda h, T0=T0: h.dma_start(out=qpB[:], in_=QPs[:, :, T0:T0 + TS2]), reads=[("QPs", st)], writes=["qpB"], dma=True)
            for sub in range(NSUB):
                A("sp", lambda h, sub=sub, T0=T0, xi=xin[sub]: h.dma_start(out=xi[:], in_=H1s[T0 + sub * 128:T0 + (sub + 1) * 128, :]),
                  reads=[("H1s", st, sub)], writes=["xin%d" % sub], dma=True)
                A("act", lambda h, xi=xin[sub], hb=h1b[sub]: h.activation(out=hb[:], in_=xi[:], func=AF.Copy),
                  reads=["xin%d" % sub], writes=["h1b%d" % sub])
            def peer_select(sub, gate, eidx, gk, ek):
                xi = xin[sub]
                xk = "xin%d" % sub
                for hc in range(16):
                    bk = hc // 4
                    A("pe", lambda h, hc=hc, bk=bk, sub=sub: h.matmul(PS[bk][:, (hc % 4) * 128:(hc % 4 + 1) * 128], lhsT=qpB[:, hc, sub * 128:(sub + 1) * 128], rhs=skT[:, hc, :],
                                                                  start=True, stop=True), reads=["qpB", "skT"], writes=[PK[bk]])
                for bk in range(4):
                    A("act", lambda h, bk=bk: h.activation(out=s_sb[:, bk * 4:(bk + 1) * 4, :], in_=PS[bk][:, :].rearrange("p (a b) -> p a b", a=4), func=AF.Copy),
                      reads=[PK[bk]], writes=[("s_sb", bk)])
                for hc in range(16):
                    sk_ = ("s_sb", hc // 4)
                    A("dve", lambda h, hc=hc: h.max(out=m16[:, hc, 0:8], in_=s_sb[:, hc, :]), reads=[sk_], writes=["m16"])
                    A("dve", lambda h, hc=hc: h.max_index(out=i16[:, hc, 0:8], in_max=m16[:, hc, 0:8], in_values=s_sb[:, hc, :]), reads=[sk_, "m16"], writes=["i16"])
                    A("dve", lambda h, hc=hc: h.match_replace(out=s2[:, 0:128], in_to_replace=m16[:, hc, 0:8], in_values=s_sb[:, hc, :], imm_value=-1e30),
                      reads=[sk_, "m16"], writes=["s2"])
                    A("dve", lambda h, hc=hc: h.max(out=m16[:, hc, 8:16], in_=s2[:, 0:128]), reads=["s2"], writes=["m16"])
                    A("dve", lambda h, hc=hc: h.max_index(out=i16[:, hc, 8:16], in_max=m16[:, hc, 8:16], in_values=s2[:, 0:128]), reads=["s2", "m16"], writes=["i16"])
                A("dve", lambda h: h.tensor_copy(out=i16f[:], in_=i16[:]), reads=["i16"], writes=["i16f"])
                m16r = m16[:].rearrange("p (h c) k -> p h c k", c=2)
                i16r = i16f[:].rearrange("p (h c) k -> p h c k", c=2)
                A("dve", lambda h: h.tensor_tensor(
                    out=cand[:].rearrange("p h (a b) -> p h a b", a=16),
                    in0=m16r[:, :, 0, :].unsqueeze(3).broadcast_to([128, 8, 16, 16]),
                    in1=m16r[:, :, 1, :].unsqueeze(2).broadcast_to([128, 8, 16, 16]), op=ALU.add), reads=["m16"], writes=["cand"])
                for hh in range(8):
                    A("dve", lambda h, hh=hh: h.max(out=b16[:, hh, 0:8], in_=cand[:, hh, :]), reads=["cand"], writes=["b16"])
                    A("dve", lambda h, hh=hh: h.max_index(out=c16[:, hh, 0:8], in_max=b16[:, hh, 0:8], in_values=cand[:, hh, :]), reads=["cand", "b16"], writes=["c16"])
                    A("dve", lambda h, hh=hh: h.match_replace(out=s2[:], in_to_replace=b16[:, hh, 0:8], in_values=cand[:, hh, :], imm_value=-1e30),
                      reads=["cand", "b16"], writes=["s2"])
                    A("dve", lambda h, hh=hh: h.max(out=b16[:, hh, 8:16], in_=s2[:]), reads=["s2"], writes=["b16"])
                    A("dve", lambda h, hh=hh: h.max_index(out=c16[:, hh, 8:16], in_max=b16[:, hh, 8:16], in_values=s2[:]), reads=["s2", "b16"], writes=["c16"])
                A("dve", lambda h: h.tensor_tensor(out=gate[:], in0=b16[:], in1=b16[:, :, 0:1].broadcast_to([128, 8, 16]), op=ALU.subtract),
                  reads=["b16"], writes=[gk])
                A("act", lambda h: h.activation(out=gate[:], in_=gate[:], func=AF.Exp), reads=[gk], writes=[gk])
                A("dve", lambda h: h.reduce_sum(out=zs[:], in_=gate[:], axis=AX.X), reads=[gk], writes=["zs"])
                A("dve", lambda h: h.reciprocal(out=zs[:], in_=zs[:]), reads=["zs"], writes=["zs"])
                A("dve", lambda h: h.tensor_tensor(out=gate[:], in0=gate[:], in1=zs[:].unsqueeze(2).broadcast_to([128, 8, 16]), op=ALU.mult),
                  reads=[gk, "zs"], writes=[gk])
                A("dve", lambda h: h.tensor_copy(out=cfa[:], in_=c16[:]), reads=["c16"], writes=["cfa"])
                A("dve", lambda h: h.tensor_tensor(out=oh[:], in0=cfa[:].unsqueeze(3).broadcast_to([128, 8, 16, 16]),
                                                   in1=thr16.unsqueeze(1).unsqueeze(1).broadcast_to([128, 8, 16, 16]), op=ALU.is_ge),
                  reads=["cfa", "iota16"], writes=[("s_sb", 0), ("s_sb", 1), ("s_sb", 2), ("s_sb", 3)])
                A("dve", lambda h: h.reduce_sum(out=cfb[:], in_=oh[:], axis=AX.X), reads=[("s_sb", 0), ("s_sb", 1), ("s_sb", 2), ("s_sb", 3)], writes=["cfb"])
                A("dve", lambda h: h.scalar_tensor_tensor(out=cfa[:], in0=cfb[:], scalar=-16.0, in1=cfa[:], op0=ALU.mult, op1=ALU.add),
                  reads=["cfa", "cfb"], writes=["cfa"])
                iob = iota16[:, 0, :].unsqueeze(1).unsqueeze(1).broadcast_to([128, 8, 16, 16])
                for (cf, cfk, cidx, eo, eok) in ((cfb, "cfb", 0, ei, "ei"), (cfa, "cfa", 1, ei2, "ei2")):
                    A("dve", lambda h, cf=cf: h.tensor_tensor(out=oh[:], in0=iob, in1=cf[:].unsqueeze(3).broadcast_to([128, 8, 16, 16]), op=ALU.is_equal),
                      reads=[cfk, "iota16"], writes=[("s_sb", 0), ("s_sb", 1), ("s_sb", 2), ("s_sb", 3)])
                    A("dve", lambda h, cidx=cidx: h.tensor_tensor(out=oh[:], in0=oh[:], in1=i16r[:, :, cidx, :].unsqueeze(2).broadcast_to([128, 8, 16, 16]), op=ALU.mult),
                      reads=[("s_sb", 0), ("s_sb", 1), ("s_sb", 2), ("s_sb", 3), "i16f"], writes=[("s_sb", 0), ("s_sb", 1), ("s_sb", 2), ("s_sb", 3)])
                    A("dve", lambda h, eo=eo: h.reduce_sum(out=eo[:], in_=oh[:], axis=AX.X), reads=[("s_sb", 0), ("s_sb", 1), ("s_sb", 2), ("s_sb", 3)], writes=[eok])
                A("dve", lambda h: h.scalar_tensor_tensor(out=ei[:], in0=ei[:], scalar=128.0, in1=ei2[:], op0=ALU.mult, op1=ALU.add),
                  reads=["ei", "ei2"], writes=["ei"])
                A("dve", lambda h: h.tensor_copy(out=eidx[:], in_=ei[:].rearrange("p a b -> p (a b)")), reads=["ei"], writes=[ek])
            def peer_slots_fn(sub, gate, eidx, gk, ek):
                xi = xin[sub]
                xk = "xin%d" % sub
                def _unused():
                    pass
                A("dve", lambda h: h.memset(actp[:], 0.0), writes=["actp"])
                A("sp", lambda h, sub=sub, T0=T0: h.dma_start(out=ffn[:], in_=Y2[T0 + sub * 128:T0 + (sub + 1) * 128, :]),
                  reads=[("Y2", st, sub, dg) for dg in range(NDG)], writes=["ffn"], dma=True)
                NBK = D // 512
                gflat = gate[:].rearrange("p a b -> p (a b)")
                pend = []

                def emit_pv(s_, gb_, gk_):
                    dg_ = diag[s_ % 3]
                    dk_ = "diag%d" % (s_ % 3)
                    A("dve", lambda h: h.tensor_scalar(
                        out=dg_[:], in0=identf[:], scalar1=wgt[:, s_:s_ + 1], scalar2=gflat[:, s_:s_ + 1], op0=ALU.mult, op1=ALU.mult),
                      reads=[("wgt", s_), gk, "identf"], writes=[dk_])
                    for j in range(NBK):
                        A("pe", lambda h, j=j: h.matmul(
                            PS[4 + j][:, :], lhsT=dg_[:], rhs=gb_[:, 1, j * 512:(j + 1) * 512], start=(s_ == 0), stop=(s_ == peer_slots - 1)),
                          reads=[dk_, gk_], writes=[PK[4 + j]])
                for s_ in range(peer_slots):
                    gi_ = gcnt[0] % NG
                    gcnt[0] += 1
                    gb_ = gbuf[gi_]
                    gk_ = "gbuf%d" % gi_
                    A("pool", lambda h, gb_=gb_, s_=s_: h.indirect_dma_start(
                        out=gb_[:].rearrange("p a d -> p (a d)"), out_offset=None, in_=UVB.rearrange("e a d -> e (a d)"),
                        in_offset=bass.IndirectOffsetOnAxis(ap=eidx[:, s_:s_ + 1], axis=0)),
                      reads=[ek] + (uvb_keys if s_ == 0 else []), writes=[gk_], dma=True)
                    A("dve", lambda h, gb_=gb_, s_=s_, xi=xi: h.scalar_tensor_tensor(
                        out=junk2[s_ % 2], in0=gb_[:, 0, :], scalar=1.0, in1=h1b[sub][:], op0=ALU.mult, op1=ALU.mult, accum_out=actp[:, s_:s_ + 1]),
                      reads=[gk_, "h1b%d" % sub, "actp"], writes=[("actp", s_), ("junkB" if s_ % 2 == 0 else "cand")])
                    A("act", lambda h, s_=s_: h.activation(out=wgt[:, s_:s_ + 1], in_=actp[:, s_:s_ + 1], func=AF.Gelu),
                      reads=[("actp", s_)], writes=[("wgt", s_)])
                    pend.append((s_, gb_, gk_))
                    if len(pend) > 1:
                        emit_pv(*pend.pop(0))
                while pend:
                    emit_pv(*pend.pop(0))
                for j in range(NBK):
                    A("dve", lambda h, j=j: h.tensor_tensor(out=ffn[:, j * 512:(j + 1) * 512], in0=ffn[:, j * 512:(j + 1) * 512], in1=PS[4 + j][:, :], op=ALU.add),
                      reads=["ffn", PK[4 + j]], writes=["ffn"])
                layer_norm(ffn, "ffn", lngB, lnbB, junk, st1B, epstB, "B")
                out_ops.append(A("sp", lambda h, sub=sub, T0=T0: h.dma_start(out=out[T0 + sub * 128:T0 + (sub + 1) * 128, :], in_=ffn[:]),
                                 reads=["ffn"], dma=True))
            for sub in range(NSUB):
                peer_select(sub, gate_l[sub], eidx_l[sub], "gate%d" % sub, "eidx%d" % sub)
            for sub in range(NSUB):
                peer_slots_fn(sub, gate_l[sub], eidx_l[sub], "gate%d" % sub, "eidx%d" % sub)
        S.finals = out_ops

        sems = {e: es.enter_context(nc.semaphore("s_" + e)) for e in ENG_NAMES}
        dsems = [es.enter_context(nc.semaphore("d%d" % i)) for i in range(N_DMA_SEMS)]
        block = es.enter_context(nc.Block())

        def mk(e):
            def body(h):
                S.emit_engine(e, h, sems, dsems)
            return body

        block.tensor(mk("pe"))
        block.scalar(mk("act"))
        block.vector(mk("dve"))
        block.gpsimd(mk("pool"))
        block.sync(mk("sp"))
    return nc


def make_consts(pos0, halo_valid):
    bf = ml_dtypes.bfloat16
    NEG = -30000.0
    k = np.arange(128)[:, None]
    q = np.arange(128)[None, :]
    kill = 0.0 if halo_valid else NEG
    prevA = np.where(k >= q, 0.0, NEG)
    cur = np.where(k <= q, 0.0, NEG)
    prevB = np.where(k > q, 0.0, NEG)
    maskA = np.stack([prevA, cur, np.minimum(prevA, kill)], axis=1).astype(np.float32)
    mB = np.stack([prevB, cur, np.minimum(prevB, kill)], axis=1)
    maskB = np.tile(mB, (1, 1, 4)).astype(np.float32)
    ident = np.eye(128, dtype=np.float32)
    rot = np.zeros((128, 128), np.float32)
    for m in range(128):
        rot[(m + 64) % 128, m] = 1.0
    cbf = np.stack([ident, rot, np.ones((128, 128), np.float32)], axis=1)
    inv = 1.0 / (10000.0 ** (np.arange(0, 128, 2, dtype=np.float32) / 128.0))
    pos = np.maximum(pos0 - HALO + np.arange(CTX), 0).astype(np.float32)
    ang = pos[None, :] * inv[:, None].astype(np.float32)
    cos = np.cos(ang).astype(np.float32)
    sin = np.sin(ang).astype(np.float32)
    cosT = np.concatenate([cos, cos], axis=0)
    sinT = np.concatenate([-sin, sin], axis=0)
    cs = np.stack([cosT, sinT], axis=1)
    io = np.arange(16, dtype=np.float32)
    iota = np.tile(np.stack([io, 16.0 * (io + 1.0)], axis=0)[None], (128, 1, 1)).astype(np.float32)
    return {
        "c_identf": ident,
        "c_bf": cbf.astype(bf),
        "c_maskA": maskA.astype(bf),
        "c_maskB": maskB.astype(bf),
        "c_cs": cs.astype(bf),
        "c_iota": iota,
    }


_PROG = {}


def kernel(x, p, w_in, sinks, w_branch_a, w_branch_b, w_out, ln1_g, ln1_b, peer_wq, peer_subkeys,
           peer_u, peer_v, ple_gate, ple_proj, ln2_g, ln2_b):
    x = np.asarray(x)
    Bn, Sq, D = x.shape
    depth = 1
    alpha = (2 * depth) ** 0.25
    n_cores = 8
    assert Bn * Sq == n_cores * TOWN and Sq == 2 * TOWN
    if D not in _PROG:
        _PROG[D] = build_program(D, alpha)
    nc = _PROG[D]
    f = lambda a: np.ascontiguousarray(np.asarray(a, dtype=np.float32))
    shared = {
        "w_in": f(w_in)[0], "sinks": f(sinks)[0].reshape(1, 8), "w_branch_a": f(w_branch_a)[0], "w_branch_b": f(w_branch_b)[0],
        "w_out": f(w_out)[0], "ln1_g": f(ln1_g)[0].reshape(1, D), "ln1_b": f(ln1_b)[0].reshape(1, D),
        "peer_wq": f(peer_wq)[0], "peer_subkeys": f(peer_subkeys)[0].reshape(16, 128, 128),
        "peer_u": f(peer_u)[0], "peer_v": f(peer_v)[0], "ple_gate": f(ple_gate)[0], "ple_proj": f(ple_proj)[0],
        "ln2_g": f(ln2_g)[0].reshape(1, D), "ln2_b": f(ln2_b)[0].reshape(1, D),
    }
    xf = f(x)
    pf = f(p)[0]
    in_maps = []
    for c in range(n_cores):
        b, half = c // 2, c % 2
        t0 = half * TOWN
        m = dict(shared)
        m["xo"] = np.ascontiguousarray(xf[b, t0:t0 + TOWN])
        m["xh"] = np.ascontiguousarray(xf[b, 0:HALO]) if half == 1 else np.zeros((HALO, D), np.float32)
        m["p"] = np.ascontiguousarray(pf[b, t0:t0 + TOWN])
        m.update(make_consts(t0, half == 1))
        in_maps.append(m)
    res = run_bass_kernel_spmd(nc, in_maps, core_ids=list(range(n_cores)))
    outs = [np.asarray(r["out"]) for r in res.results]
    full = np.zeros((Bn, Sq, D), np.float32)
    for c in range(n_cores):
        b, half = c // 2, c % 2
        full[b, half * TOWN:(half + 1) * TOWN] = outs[c]
    return full
```

```python
import math
from contextlib import ExitStack

import ml_dtypes
import numpy as np

import concourse.bass as bass
import concourse.mybir as mybir
from concourse.bass_utils import run_bass_kernel_spmd

F32 = mybir.dt.float32
BF16 = mybir.dt.bfloat16
U32 = mybir.dt.uint32
AF = mybir.ActivationFunctionType
ALU = mybir.AluOpType
AX = mybir.AxisListType

ENG_NAMES = ["pe", "act", "dve", "pool", "sp"]
N_DMA_SEMS = 56
N_SW_SEMS = 16


class Op:
    __slots__ = ("eng", "fn", "deps", "signal", "is_dma", "dma_slot", "dma_val", "sigval")

    def __init__(self, eng, fn, is_dma):
        self.eng = eng
        self.fn = fn
        self.deps = []
        self.signal = False
        self.is_dma = is_dma
        self.dma_slot = None
        self.dma_val = None
        self.sigval = None


class Sched:
    def __init__(self):
        self.ops = {e: [] for e in ENG_NAMES}
        self.last_w = {}
        self.readers = {}
        self.n_dma = 0
        self.n_dma_sw = 0
        self.dma_hist = {}
        self.finals = []
        self._fin = False

    def add(self, eng, fn, reads=(), writes=(), dma=False):
        def flat(ks):
            o = []
            for k_ in ks:
                if isinstance(k_, list):
                    o.extend(k_)
                else:
                    o.append(k_)
            return o
        reads = flat(reads)
        writes = flat(writes)
        op = Op(eng, fn, dma)
        deps = []
        for r in reads:
            w = self.last_w.get(r)
            if w is not None:
                deps.append(w)
        for r in writes:
            w = self.last_w.get(r)
            if w is not None:
                deps.append(w)
            deps.extend(self.readers.get(r, ()))
        if dma:
            if eng == "pool":
                k = self.n_dma_sw
                self.n_dma_sw += 1
                op.dma_slot = k % N_SW_SEMS
                op.dma_val = 16 * (k // N_SW_SEMS + 1)
            else:
                k = self.n_dma
                self.n_dma += 1
                op.dma_slot = N_SW_SEMS + k % (N_DMA_SEMS - N_SW_SEMS)
                op.dma_val = 16 * (k // (N_DMA_SEMS - N_SW_SEMS) + 1)
            prev = self.dma_hist.get(op.dma_slot)
            if prev is not None:
                deps.append(prev)
            self.dma_hist[op.dma_slot] = op
        fp = getattr(self, "fence_pending", None)
        if fp and fp.get(eng):
            deps.extend(fp[eng])
            fp[eng] = []
        seen = set()
        for d in deps:
            if d is op or id(d) in seen:
                continue
            seen.add(id(d))
            if d.eng == "pe" and eng == "pe" and not d.is_dma and not dma:
                continue
            op.deps.append(d)
            d.signal = True
        self.ops[eng].append(op)
        for r in reads:
            self.readers.setdefault(r, []).append(op)
        for r in writes:
            self.last_w[r] = op
            self.readers[r] = []
        return op

    def fence(self):
        deps = []
        for e in ENG_NAMES:
            last = None
            for op in reversed(self.ops[e]):
                if not op.is_dma:
                    last = op
                    break
            if last is not None:
                deps.append(last)
        deps.extend(self.dma_hist.values())
        self.fence_pending = {e: list(deps) for e in ENG_NAMES}

    def finalize(self):
        if self._fin:
            return
        self._fin = True
        for e in ENG_NAMES:
            c = 0
            for op in self.ops[e]:
                if op.is_dma:
                    continue
                if op.signal:
                    c += 1
                    op.sigval = c

    def emit_engine(self, e, h, sems, dma_sems):
        self.finalize()
        waited = {}

        def wait_for(d):
            if d.is_dma:
                key = ("dma", d.dma_slot)
                val = d.dma_val
                sem = dma_sems[d.dma_slot]
            else:
                key = d.eng
                val = d.sigval
                sem = sems[d.eng]
            if waited.get(key, 0) >= val:
                return
            waited[key] = val
            h.wait_ge(sem, val)

        for op in self.ops[e]:
            for d in op.deps:
                wait_for(d)
            ins = op.fn(h)
            if op.is_dma:
                ins.then_inc(dma_sems[op.dma_slot], 16)
            elif op.signal:
                ins.then_inc(sems[e], 1)
        if e == "sp":
            for d in self.finals:
                wait_for(d)


HD = 128
TOWN = 2048
HALO = 2048
CTX = TOWN + HALO
A_PATTERNS = ((128, 1), (512, 4), (2048, 16))
N_EXP = 16384
PLE = 256
LN_EPS = 1e-5


def build_program(D, depth_alpha, n_super=None, peer_slots=128):
    KC = D // 128
    NDG = D // 512 if D >= 512 else 1
    DG = min(512, D)
    IN_COLS = 6144 + 2 * D
    TS = 512
    TS2 = 256
    NSUB = TS2 // 128
    NSUP = TOWN // TS2 if n_super is None else n_super
    alpha = float(depth_alpha)
    scale = 1.0 / math.sqrt(HD)

    nc = bass.Bass("TRN2", target_bir_lowering=False)

    def din(name, shape, dt=F32):
        return nc.dram_tensor(name, list(shape), dt, kind="ExternalInput").ap()

    def dscr(name, shape, dt):
        return nc.dram_tensor(name, list(shape), dt, kind="Internal").ap()

    xo = din("xo", [TOWN, D])
    xh = din("xh", [HALO, D])
    pin = din("p", [TOWN, PLE])
    w_in = din("w_in", [D, IN_COLS])
    sinks = din("sinks", [1, 8])
    w_ba = din("w_branch_a", [512, D])
    w_bb = din("w_branch_b", [1024, D])
    w_out = din("w_out", [D, D])
    ln1_g = din("ln1_g", [1, D])
    ln1_b = din("ln1_b", [1, D])
    peer_wq = din("peer_wq", [D, 2048])
    peer_sk = din("peer_subkeys", [16, 128, 128])
    peer_u = din("peer_u", [N_EXP, D])
    peer_v = din("peer_v", [N_EXP, D])
    ple_gate = din("ple_gate", [D, D])
    ple_proj = din("ple_proj", [PLE, D])
    ln2_g = din("ln2_g", [1, D])
    ln2_b = din("ln2_b", [1, D])
    c_identf = din("c_identf", [128, 128])
    c_bf = din("c_bf", [128, 3, 128], BF16)
    c_maskA = din("c_maskA", [128, 3, 128], BF16)
    c_maskB = din("c_maskB", [128, 3, 512], BF16)
    c_cs = din("c_cs", [128, 2, CTX], BF16)
    c_iota = din("c_iota", [128, 2, 16])

    out = nc.dram_tensor("out", [TOWN, D], F32, kind="ExternalOutput").ap()

    QT = dscr("QT", [20, 128, TOWN], BF16)
    KT = dscr("KT", [14, 128, CTX], BF16)
    VT = dscr("VT", [CTX, 14, 128], BF16)
    OT = dscr("OT", [12, 128, TOWN], BF16)
    Y2 = dscr("Y2", [TOWN, D], F32)
    UVB = dscr("UVB", [N_EXP, 2, D], BF16)
    H1s = dscr("H1s", [TOWN, D], F32)
    QPs = dscr("QPs", [128, 16, TOWN], BF16)
    WGb = dscr("WGb", [D, 2 * D], BF16)
    WAb = dscr("WAb", [512, D], BF16)
    WBb = dscr("WBb", [1024, D], BF16)
    WOb = dscr("WOb", [D, D], BF16)
    WQb = dscr("WQb", [D, 2048], BF16)
    WPGb = dscr("WPGb", [D, D], BF16)
    WPPb = dscr("WPPb", [PLE, D], BF16)

    S = Sched()
    A = S.add

    with ExitStack() as es:
        def sb_raw(name, shape, dt):
            return es.enter_context(nc.sbuf_tensor(name, list(shape), dt))

        ARENA_F32 = 198 * 256
        arena_t = sb_raw("arena", [128, ARENA_F32], F32)
        aoff = [0]

        def areset():
            S.fence()
            aoff[0] = 0

        def sb(name, shape, dt):
            n = 1
            for d_ in shape[1:]:
                n *= d_
            esz = 4 if dt in (F32, U32) else 2
            nf = (n * esz + 3) // 4
            assert aoff[0] + nf <= ARENA_F32, ("arena overflow", name, aoff[0], nf)
            v = arena_t[:, aoff[0]:aoff[0] + nf]
            aoff[0] += nf
            if dt != F32:
                v = v.bitcast(dt)
            v = v[:, 0:n]
            if len(shape) == 3:
                v = v.rearrange("p (a b) -> p a b", a=shape[1])
            elif len(shape) == 4:
                v = v.rearrange("p (a b c) -> p a b c", a=shape[1], b=shape[2])
            return v

        PS = [es.enter_context(nc.psum_tensor("ps%d" % i, [128, 512], F32)) for i in range(8)]
        PK = ["ps%d" % i for i in range(8)]

        identf = sb_raw("identf", [128, 128], F32)
        cbf = sb_raw("cbf", [128, 3, 128], BF16)
        maskA = sb_raw("maskA", [128, 3, 128], BF16)
        maskB = sb_raw("maskB", [128, 3, 512], BF16)
        iota16 = sb_raw("iota16", [128, 2, 16], F32)
        thr16 = iota16[:, 1, :]
        esk = sb_raw("esk", [128, 8], F32)
        esinkB = sb_raw("esinkB", [128, 2, 4, 128], F32)
        A("sp", lambda h: h.dma_start(out=identf[:], in_=c_identf[:, :]), writes=["identf"], dma=True)
        A("sp", lambda h: h.dma_start(out=cbf[:], in_=c_bf[:, :, :]), writes=["cbf"], dma=True)
        A("sp", lambda h: h.dma_start(out=maskA[:], in_=c_maskA[:, :, :]), writes=["maskA"], dma=True)
        A("sp", lambda h: h.dma_start(out=maskB[:], in_=c_maskB[:, :, :]), writes=["maskB"], dma=True)
        A("sp", lambda h: h.dma_start(out=iota16[:], in_=c_iota[:, :, :]), writes=["iota16"], dma=True)
        A("sp", lambda h: h.dma_start(out=esk[:], in_=sinks.partition_broadcast(128)), writes=["esk"], dma=True)
        A("act", lambda h: h.activation(out=esk[:], in_=esk[:], func=AF.Exp), reads=["esk"], writes=["esk"])
        for kv in range(2):
            A("dve", lambda h, kv=kv: h.tensor_copy(
                out=esinkB[:, kv, :, :], in_=esk[:, 4 * kv:4 * kv + 4].unsqueeze(2).broadcast_to([128, 4, 128])),
              reads=["esk"], writes=["esinkB"])
        identb = cbf[:, 0, :]
        rotp = cbf[:, 1, :]
        onesb = cbf[:, 2, :]

        cs = sb("cs", [128, 2, CTX], BF16)
        A("sp", lambda h: h.dma_start(out=cs[:], in_=c_cs[:, :, :]), writes=["cs"], dma=True)
        xin, xT, wbig = [], [], []

        def alloc_shared(n_xin, ts):
            xin[:] = [sb("xin%d" % i, [128, D], F32) for i in range(n_xin)]
            xT[:] = [sb("xT%d" % i, [128, KC, ts], BF16) for i in range(2)]
            wbig[:] = [sb("wbig%d" % i, [128, KC, DG], BF16) for i in range(2)]

        alloc_shared(4, TS)
        wcnt = [0]
        CAST_ROWS = 256
        cast_pending = [(t_, c_) for c_ in range(N_EXP // CAST_ROWS) for t_ in range(2)]
        uvb_keys = [("UVB", t_, c_) for (t_, c_) in cast_pending]
        tcf, tcb = [], []
        tc_cnt = [0]
        tc_store = []

        def flush_cast_store():
            while tc_store:
                tb_, key_, r0, t_, kk = tc_store.pop(0)
                A("sp", lambda h, tb_=tb_, r0=r0, t_=t_: h.dma_start(
                    out=UVB[r0:r0 + CAST_ROWS, t_, :].rearrange("(p a) d -> p a d", a=2), in_=tb_[:]),
                  reads=[key_], writes=[kk], dma=True)

        def emit_cast(n):
            for _ in range(n):
                if not cast_pending:
                    flush_cast_store()
                    return
                t_, c_ = cast_pending.pop(0)
                i = tc_cnt[0] % 2
                tc_cnt[0] += 1
                srct = peer_u if t_ == 0 else peer_v
                r0 = c_ * CAST_ROWS
                tf_, tb_ = tcf[i], tcb[i]
                A("sp", lambda h, tf_=tf_, srct=srct, r0=r0: h.dma_start(
                    out=tf_[:], in_=srct[r0:r0 + CAST_ROWS, :].rearrange("(p a) d -> p a d", a=2)),
                  writes=["tcf%d" % i], dma=True)
                flush_cast_store()
                A("act", lambda h, tf_=tf_, tb_=tb_: h.activation(out=tb_[:], in_=tf_[:], func=AF.Copy),
                  reads=["tcf%d" % i], writes=["tcb%d" % i])
                tc_store.append((tb_, "tcb%d" % i, r0, t_, ("UVB", t_, c_)))

        def load_w(src_ap, rows_kc, cols, key_extra=None):
            i = wcnt[0] % 2
            wcnt[0] += 1
            buf = wbig[i]
            view = buf[:, 0:rows_kc, 0:cols]
            A("pool", lambda h: h.dma_start(out=view, in_=src_ap.rearrange("(kc p) n -> p kc n", p=128)),
              writes=["wbig%d" % i], dma=True)
            return view, "wbig%d" % i

        def load_wb(src_ap, rows_kc, cols, name):
            i = wcnt[0] % 2
            wcnt[0] += 1
            view = wbig[i][:, 0:rows_kc, 0:cols]
            kl = ["wbig%d" % i]
            A("sp", lambda h: h.dma_start(out=view, in_=src_ap.rearrange("(kc p) n -> p kc n", p=128)),
              reads=wb_keys[name], writes=[kl], dma=True)
            return view, kl

        tcnt = [0]

        def load_transpose(src_rows_ap_fn, nsub, xTbuf, xTkey, keep=None, ncols=D, xin_list=None, xkeys=None):
            kcn = ncols // 128
            for sub in range(nsub):
                if xin_list is None:
                    xi = xin[sub % len(xin)]
                    xk = "xin%d" % (sub % len(xin))
                else:
                    xi = xin_list[sub]
                    xk = xkeys[sub]
                if src_rows_ap_fn is not None:
                    A("sp", lambda h, xi=xi, sub=sub: h.dma_start(out=xi[:, 0:ncols], in_=src_rows_ap_fn(sub)),
                      writes=[xk], dma=True)
                for k0 in range(0, kcn, 4):
                    n4 = min(4, kcn - k0)
                    bi = tcnt[0] % 2
                    tcnt[0] += 1
                    for j in range(n4):
                        A("pe", lambda h, xi=xi, bi=bi, j=j, k0=k0: h.transpose(
                            out=PS[bi][:, j * 128:(j + 1) * 128], in_=xi[:, (k0 + j) * 128:(k0 + j + 1) * 128],
                            identity=identf[:]), reads=[xk, "identf"], writes=[PK[bi]])
                    A("act", lambda h, bi=bi, n4=n4, k0=k0, sub=sub: h.activation(
                        out=xTbuf[:, k0:k0 + n4, sub * 128:(sub + 1) * 128],
                        in_=PS[bi][:, 0:n4 * 128].rearrange("p (a b) -> p a b", a=n4), func=AF.Copy),
                      reads=[PK[bi]], writes=[xTkey])

        qraw = [sb("qraw%d" % i, [128, TS], BF16) for i in range(2)]
        t1 = [sb("t1_%d" % i, [128, TS], F32) for i in range(2)]
        t2 = [sb("t2_%d" % i, [128, TS], F32) for i in range(2)]
        qout = [sb("qout%d" % i, [128, TS], BF16) for i in range(3)]
        vout = [sb("vout%d" % i, [128, 512], BF16) for i in range(3)]
        ccnt = [0]
        vcnt = [0]
        qk_written = {}
        v_written = {}

        NT_H = HALO // TS
        for ti in range(NT_H + TOWN // TS):
            halo = ti < NT_H
            src = xh if halo else xo
            row0 = (ti % NT_H) * TS if halo else (ti - NT_H) * TS
            ctx0 = ti * TS
            xTb = xT[ti % 2]
            xTk = "xT%d" % (ti % 2)
            load_transpose(lambda sub, src=src, row0=row0: src[row0 + sub * 128: row0 + (sub + 1) * 128, :], 4, xTb, xTk)
            if halo and ti < NT_H - 1:
                groups = [5, 8]
            elif halo:
                groups = [3, 4, 5, 6, 7, 8, 11]
            else:
                groups = list(range(12))
            for gi in groups:
                wv, wk = load_w(w_in[:, gi * 512:(gi + 1) * 512], KC, 512)
                chunks = []
                vparts = []
                if gi < 3:
                    chunks = [("q", gi * 4 + c, c) for c in range(4)]
                elif gi < 6:
                    chunks = [("k", (gi - 3) * 4 + c, c) for c in range(4)]
                elif gi < 9:
                    vparts = [(0, 512, (gi - 6) * 4, 4)]
                elif gi < 11:
                    chunks = [("q", 12 + (gi - 9) * 4 + c, c) for c in range(4)]
                else:
                    chunks = [("k", 12, 0), ("k", 13, 1)]
                    vparts = [(256, 256, 12, 2)]
                for (typ, head, c) in chunks:
                    i = ccnt[0]
                    ccnt[0] += 1
                    pb = 2 + (i % 3)
                    pw = 5 + (i % 2)
                    for kc in range(KC):
                        A("pe", lambda h, pb=pb, kc=kc, c=c, wv=wv, xTb=xTb: h.matmul(
                            PS[pb][:, :], lhsT=wv[:, kc, c * 128:(c + 1) * 128], rhs=xTb[:, kc, :],
                            start=(kc == 0), stop=(kc == KC - 1)), reads=[wk, xTk], writes=[PK[pb]])
                    qr = qraw[i % 2]
                    qrk = "qraw%d" % (i % 2)
                    A("act", lambda h, pb=pb, qr=qr: h.activation(out=qr[:], in_=PS[pb][:, :], func=AF.Copy),
                      reads=[PK[pb]], writes=[qrk])
                    A("pe", lambda h, pw=pw, qr=qr: h.matmul(PS[pw][:, :], lhsT=rotp, rhs=qr[:], start=True, stop=True),
                      reads=[qrk, "cbf"], writes=[PK[pw]])
                    tt1, tt2 = t1[i % 2], t2[i % 2]
                    A("dve", lambda h, tt1=tt1, qr=qr, ctx0=ctx0: h.tensor_tensor(
                        out=tt1[:], in0=qr[:], in1=cs[:, 0, ctx0:ctx0 + TS], op=ALU.mult),
                      reads=[qrk, "cs"], writes=["t1_%d" % (i % 2)])
                    A("dve", lambda h, tt2=tt2, pw=pw, ctx0=ctx0: h.tensor_tensor(
                        out=tt2[:], in0=PS[pw][:, :], in1=cs[:, 1, ctx0:ctx0 + TS], op=ALU.mult),
                      reads=[PK[pw], "cs"], writes=["t2_%d" % (i % 2)])
                    qo = qout[i % 3]
                    qok = "qout%d" % (i % 3)
                    A("dve", lambda h, qo=qo, tt1=tt1, tt2=tt2: h.tensor_tensor(out=qo[:], in0=tt1[:], in1=tt2[:], op=ALU.add),
                      reads=["t1_%d" % (i % 2), "t2_%d" % (i % 2)], writes=[qok])
                    if typ == "q":
                        dst = QT[head, :, ctx0 - HALO:ctx0 - HALO + TS]
                        key = ("QT", head, ti)
                        qk_written.setdefault(("QT", head), []).append(key)
                    else:
                        dst = KT[head, :, ctx0:ctx0 + TS]
                        key = ("KT", head, ti)
                        qk_written.setdefault(("KT", head), []).append(key)
                    A("sp", lambda h, dst=dst, qo=qo: h.dma_start(out=dst, in_=qo[:]), reads=[qok], writes=[key], dma=True)
                for (c0, ncol, head0, nh) in vparts:
                    for sub in range(4):
                        i = vcnt[0]
                        vcnt[0] += 1
                        pb = 2 + (ccnt[0] + i) % 3
                        ccnt[0] += 0
                        pb = 7
                        for kc in range(KC):
                            A("pe", lambda h, pb=pb, kc=kc, sub=sub, wv=wv, xTb=xTb, c0=c0, ncol=ncol: h.matmul(
                                PS[pb][:, 0:ncol], lhsT=xTb[:, kc, sub * 128:(sub + 1) * 128], rhs=wv[:, kc, c0:c0 + ncol],
                                start=(kc == 0), stop=(kc == KC - 1)), reads=[wk, xTk], writes=[PK[pb]])
                        vo = vout[i % 3]
                        vok = "vout%d" % (i % 3)
                        A("act", lambda h, pb=pb, vo=vo, ncol=ncol: h.activation(out=vo[:, 0:ncol], in_=PS[pb][:, 0:ncol], func=AF.Copy),
                          reads=[PK[pb]], writes=[vok])
                        r0 = ctx0 + sub * 128
                        key = ("VT", head0, ti, sub)
                        for hh in range(nh):
                            v_written.setdefault(("VT", head0 + hh), []).append(key)
                        A("sp", lambda h, vo=vo, r0=r0, head0=head0, nh=nh, ncol=ncol: h.dma_start(
                            out=VT[r0:r0 + 128, head0:head0 + nh, :],
                            in_=vo[:, 0:ncol].rearrange("p (a b) -> p a b", a=nh)), reads=[vok], writes=[key], dma=True)

        wb_keys = {}

        def precast(name, dst, src, rows, cols):
            keys = []
            for c0 in range(0, cols, 2048):
                cw = min(2048, cols - c0)
                for r0 in range(0, rows, 1024):
                    rw = min(1024, rows - r0)
                    k_ = ("WB", name, c0, r0)
                    keys.append(k_)
                    A("pool", lambda h, c0=c0, cw=cw, r0=r0, rw=rw: h.dma_start(out=dst[r0:r0 + rw, c0:c0 + cw], in_=src[r0:r0 + rw, c0:c0 + cw]),
                      writes=[k_], dma=True)
            wb_keys[name] = keys

        precast("WG", WGb, w_in[:, 6144:6144 + 2 * D], D, 2 * D)
        precast("WA", WAb, w_ba, 512, D)
        precast("WB", WBb, w_bb, 1024, D)
        precast("WO", WOb, w_out, D, D)
        precast("WQ", WQb, peer_wq, D, 2048)
        precast("WPG", WPGb, ple_gate, D, D)
        precast("WPP", WPPb, ple_proj, PLE, D)
        areset()
        qt = [sb("qt%d" % i, [128, TOWN], BF16) for i in range(2)]
        kt = [sb("kt%d" % i, [128, CTX], BF16) for i in range(2)]
        vp = [sb("vp%d" % i, [128, CTX], BF16) for i in range(2)]
        acc = sb("acc", [128, 2, TOWN], F32)
        rden = sb("rden", [128, TOWN], F32)
        oas = sb("oas", [128, TOWN], BF16)
        pT = [sb("pT%d" % i, [128, 256], BF16) for i in range(3)]
        hcnt = 0
        bcnt = 0
        for slot in range(4):
            for g in range(3):
                head = g * 4 + slot
                dil = A_PATTERNS[g][1]
                halo_g = 128 * dil
                nblk = TOWN // (128 * dil)
                hb = hcnt % 2
                hcnt += 1
                qtb, ktb, vpb = qt[hb], kt[hb], vp[hb]
                qk_, kk_, vk_ = "qt%d" % hb, "kt%d" % hb, "vp%d" % hb
                clen = halo_g + TOWN
                A("sp", lambda h, qtb=qtb, head=head: h.dma_start(out=qtb[:], in_=QT[head, :, :]),
                  reads=qk_written[("QT", head)], writes=[qk_], dma=True)
                A("sp", lambda h, ktb=ktb, head=head, clen=clen: h.dma_start(out=ktb[:, 0:clen], in_=KT[head, :, CTX - clen:CTX]),
                  reads=qk_written[("KT", head)], writes=[kk_], dma=True)
                vview = vpb[:, 0:clen].rearrange("p (b r d) -> p b r d", r=dil, d=128) if False else None
                nB = nblk + 1
                vdst = vpb[:, 0:nB * dil * 128].rearrange("p (b r d) -> p b r d", b=nB, r=dil)
                vsrc = VT[CTX - clen:CTX, head, :].rearrange("(b i r) d -> i b r d", b=nB, i=128, r=dil)
                for B_ in range(nB):
                    A("sp", lambda h, vdst=vdst, vsrc=vsrc, B_=B_: h.dma_start(out=vdst[:, B_, :, :], in_=vsrc[:, B_, :, :]),
                      reads=v_written[("VT", head)], writes=[vk_], dma=True)
                qv = qtb[:].rearrange("p (n r) -> p r n", r=dil)
                kv_ = ktb[:, 0:clen].rearrange("p (n r) -> p r n", r=dil)
                accv = acc[:].rearrange("p a (n r) -> p a r n", r=dil)
                for r in range(dil):
                    for b in range(nblk):
                        i = bcnt
                        bcnt += 1
                        sbk = i % 3
                        obk = 3 + i % 3
                        qs = qv[:, r, b * 128:(b + 1) * 128]
                        kp = kv_[:, r, b * 128:(b + 1) * 128]
                        kc_ = kv_[:, r, (b + 1) * 128:(b + 2) * 128]
                        mprev = maskA[:, 2, :] if b == 0 else maskA[:, 0, :]
                        A("pe", lambda h, sbk=sbk, kp=kp, qs=qs: h.matmul(PS[sbk][:, 0:128], lhsT=kp, rhs=qs, start=True, stop=False),
                          reads=[qk_, kk_], writes=[PK[sbk]])
                        A("pe", lambda h, sbk=sbk, mprev=mprev: h.matmul(PS[sbk][:, 0:128], lhsT=identb, rhs=mprev, start=False, stop=True),
                          reads=["cbf", "maskA"], writes=[PK[sbk]])
                        A("pe", lambda h, sbk=sbk, kc_=kc_, qs=qs: h.matmul(PS[sbk][:, 128:256], lhsT=kc_, rhs=qs, start=True, stop=False),
                          reads=[qk_, kk_], writes=[PK[sbk]])
                        A("pe", lambda h, sbk=sbk: h.matmul(PS[sbk][:, 128:256], lhsT=identb, rhs=maskA[:, 1, :], start=False, stop=True),
                          reads=["cbf", "maskA"], writes=[PK[sbk]])
                        pt = pT[i % 3]
                        ptk = "pT%d" % (i % 3)
                        A("act", lambda h, pt=pt, sbk=sbk: h.activation(out=pt[:], in_=PS[sbk][:, 0:256], func=AF.Exp, scale=scale),
                          reads=[PK[sbk]], writes=[ptk])
                        A("pe", lambda h, obk=obk, pt=pt, vdst=vdst, b=b, r=r: h.matmul(
                            PS[obk][:, 0:128], lhsT=vdst[:, b, r, :], rhs=pt[:, 0:128], start=True, stop=False),
                          reads=[ptk, vk_], writes=[PK[obk]])
                        A("pe", lambda h, obk=obk, pt=pt, vdst=vdst, b=b, r=r: h.matmul(
                            PS[obk][:, 0:128], lhsT=vdst[:, b + 1, r, :], rhs=pt[:, 128:256], start=False, stop=True),
                          reads=[ptk, vk_], writes=[PK[obk]])
                        A("pe", lambda h, obk=obk, pt=pt: h.matmul(PS[obk][:, 128:256], lhsT=onesb, rhs=pt[:, 0:128], start=True, stop=False),
                          reads=[ptk, "cbf"], writes=[PK[obk]])
                        A("pe", lambda h, obk=obk, pt=pt: h.matmul(PS[obk][:, 128:256], lhsT=onesb, rhs=pt[:, 128:256], start=False, stop=True),
                          reads=[ptk, "cbf"], writes=[PK[obk]])
                        av = accv[:, :, r, b * 128:(b + 1) * 128]
                        ov = PS[obk][:, 0:256].rearrange("p (a q) -> p a q", a=2)
                        akey = "acc"
                        if g == 0:
                            A("dve", lambda h, av=av, ov=ov: h.tensor_copy(out=av, in_=ov), reads=[PK[obk]], writes=[akey])
                        else:
                            A("dve", lambda h, av=av, ov=ov: h.tensor_tensor(out=av, in0=av, in1=ov, op=ALU.add),
                              reads=[PK[obk], akey], writes=[akey])
            akeys = ["acc"]
            A("dve", lambda h: h.reciprocal(out=rden[:], in_=acc[:, 1, :]), reads=akeys, writes=["rden"])
            A("dve", lambda h: h.tensor_tensor(out=oas[:], in0=acc[:, 0, :], in1=rden[:], op=ALU.mult),
              reads=akeys + ["rden"], writes=["oas"])
            A("sp", lambda h, slot=slot: h.dma_start(out=OT[slot, :, :], in_=oas[:]), reads=["oas"], writes=[("OT", slot)], dma=True)

        qtB = sb("qtB", [128, 4, TOWN], BF16)
        obuf = sb("obuf", [128, 4, TOWN], BF16)
        pTB = [sb("pTB%d" % i, [128, 2, 512], BF16) for i in range(2)]
        denB = sb("denB", [128, 512], F32)
        for kvh in range(2):
            hb = hcnt % 2
            hcnt += 1
            ktb, vpb = kt[hb], vp[hb]
            kk_, vk_ = "kt%d" % hb, "vp%d" % hb
            clen = 128 + TOWN
            nB = TOWN // 128 + 1
            for hh in range(4):
                A("sp", lambda h, hh=hh, kvh=kvh: h.dma_start(out=qtB[:, hh, :], in_=QT[12 + 4 * kvh + hh, :, :]),
                  reads=qk_written[("QT", 12 + 4 * kvh + hh)], writes=["qtB"], dma=True)
            A("sp", lambda h, ktb=ktb, kvh=kvh, clen=clen: h.dma_start(out=ktb[:, 0:clen], in_=KT[12 + kvh, :, CTX - clen:CTX]),
              reads=qk_written[("KT", 12 + kvh)], writes=[kk_], dma=True)
            vdst = vpb[:, 0:nB * 128].rearrange("p (b d) -> p b d", b=nB)
            vsrc = VT[CTX - clen:CTX, 12 + kvh, :].rearrange("(b i) d -> i b d", i=128)
            A("sp", lambda h, vdst=vdst, vsrc=vsrc: h.dma_start(out=vdst, in_=vsrc),
              reads=v_written[("VT", 12 + kvh)], writes=[vk_], dma=True)
            for b in range(TOWN // 128):
                j = b % 2
                b0, b1, b2, b3 = 4 * j, 4 * j + 1, 4 * j + 2, 4 * j + 3
                qs = qtB[:, :, b * 128:(b + 1) * 128]
                mprev = maskB[:, 2, :] if b == 0 else maskB[:, 0, :]
                A("pe", lambda h, b0=b0, b=b, qs=qs, ktb=ktb: h.matmul(PS[b0][:, :], lhsT=ktb[:, b * 128:(b + 1) * 128], rhs=qs, start=True, stop=False),
                  reads=["qtB", kk_], writes=[PK[b0]])
                A("pe", lambda h, b0=b0, mprev=mprev: h.matmul(PS[b0][:, :], lhsT=identb, rhs=mprev, start=False, stop=True),
                  reads=["cbf", "maskB"], writes=[PK[b0]])
                A("pe", lambda h, b1=b1, b=b, qs=qs, ktb=ktb: h.matmul(PS[b1][:, :], lhsT=ktb[:, (b + 1) * 128:(b + 2) * 128], rhs=qs, start=True, stop=False),
                  reads=["qtB", kk_], writes=[PK[b1]])
                A("pe", lambda h, b1=b1: h.matmul(PS[b1][:, :], lhsT=identb, rhs=maskB[:, 1, :], start=False, stop=True),
                  reads=["cbf", "maskB"], writes=[PK[b1]])
                pt = pTB[j]
                ptk = "pTB%d" % j
                A("act", lambda h, pt=pt, b0=b0: h.activation(out=pt[:, 0, :], in_=PS[b0][:, :], func=AF.Exp, scale=scale),
                  reads=[PK[b0]], writes=[ptk + "a"])
                A("act", lambda h, pt=pt, b1=b1: h.activation(out=pt[:, 1, :], in_=PS[b1][:, :], func=AF.Exp, scale=scale),
                  reads=[PK[b1]], writes=[ptk + "b"])
                A("pe", lambda h, b2=b2, pt=pt, vdst=vdst, b=b: h.matmul(PS[b2][:, :], lhsT=vdst[:, b, :], rhs=pt[:, 0, :], start=True, stop=False),
                  reads=[ptk + "a", vk_], writes=[PK[b2]])
                A("pe", lambda h, b2=b2, pt=pt, vdst=vdst, b=b: h.matmul(PS[b2][:, :], lhsT=vdst[:, b + 1, :], rhs=pt[:, 1, :], start=False, stop=True),
                  reads=[ptk + "b", vk_], writes=[PK[b2]])
                A("pe", lambda h, b3=b3, pt=pt: h.matmul(PS[b3][:, :], lhsT=onesb, rhs=pt[:, 0, :], start=True, stop=False),
                  reads=[ptk + "a", "cbf"], writes=[PK[b3]])
                A("pe", lambda h, b3=b3, pt=pt: h.matmul(PS[b3][:, :], lhsT=onesb, rhs=pt[:, 1, :], start=False, stop=True),
                  reads=[ptk + "b", "cbf"], writes=[PK[b3]])
                A("dve", lambda h, b3=b3, kvh=kvh: h.tensor_tensor(
                    out=denB[:], in0=PS[b3][:, :], in1=esinkB[:, kvh, :, :].rearrange("p a b -> p (a b)"), op=ALU.add),
                  reads=[PK[b3], "esinkB"], writes=["denB"])
                A("dve", lambda h: h.reciprocal(out=denB[:], in_=denB[:]), reads=["denB"], writes=["denB"])
                A("dve", lambda h, b2=b2, b=b: h.tensor_tensor(
                    out=obuf[:, :, b * 128:(b + 1) * 128], in0=PS[b2][:, :].rearrange("p (a q) -> p a q", a=4),
                    in1=denB[:].rearrange("p (a q) -> p a q", a=4), op=ALU.mult),
                  reads=[PK[b2], "denB"], writes=["obuf"])
            for hh in range(4):
                A("sp", lambda h, hh=hh, kvh=kvh: h.dma_start(out=OT[4 + 4 * kvh + hh, :, :], in_=obuf[:, hh, :]),
                  reads=["obuf"], writes=[("OT", 4 + 4 * kvh + hh)], dma=True)

        areset()
        alloc_shared(NSUB, TS2)
        oat = sb("oat", [128, 12, TS2], BF16)
        mq = sb("mq", [128, max(KC, 16), TS2], BF16)
        mergedT = mq[:, 0:KC, :]
        qpT = mq[:, 0:16, :]
        wsm_a = sb("wsm_a", [128, 4, DG], BF16)
        wsm_b = sb("wsm_b", [128, 8, DG], BF16)
        wpp = sb("wpp", [128, 2, D], BF16)
        ppT = sb("ppT", [128, 2, TS2], BF16)
        sg = [sb("sg%d" % i, [128, 512], F32) for i in range(2)]
        mt = [sb("mt%d" % i, [128, 512], F32) for i in range(2)]
        lngA = sb("lngA", [128, D], F32)
        lnbA = sb("lnbA", [128, D], F32)
        junkA = sb("junkA", [128, D], BF16)
        st1A = sb("st1A", [128, 8], F32)
        epstA = sb("epstA", [128, 8], F32)
        pstg = sb("pstg", [128, NSUB * PLE], F32)
        tcf[:] = [sb("tcf%d" % i, [128, 2, D], F32) for i in range(2)]
        tcb[:] = [sb("tcb%d" % i, [128, 2, D], BF16) for i in range(2)]
        A("dve", lambda h: h.memset(epstA[:], LN_EPS), writes=["epstA"])
        A("sp", lambda h: h.dma_start(out=wpp[:], in_=WPPb.rearrange("(kc p) n -> p kc n", p=128)), reads=wb_keys["WPP"], writes=["wpp"], dma=True)
        A("sp", lambda h: h.dma_start(out=lngA[:], in_=ln1_g.partition_broadcast(128)), writes=["lngA"], dma=True)
        A("sp", lambda h: h.dma_start(out=lnbA[:], in_=ln1_b.partition_broadcast(128)), writes=["lnbA"], dma=True)

        gcnt = [0]
        out_ops = []

        def layer_norm(xi, xk, lng, lnb, junk, st1, epst, tg):
            A("dve", lambda h: h.reduce_sum(out=st1[:, 0:1], in_=xi[:], axis=AX.X), reads=[xk], writes=["st1a" + tg])
            A("dve", lambda h: h.tensor_scalar(out=st1[:, 1:2], in0=st1[:, 0:1], scalar1=-1.0 / D, scalar2=None, op0=ALU.mult),
              reads=["st1a" + tg], writes=["st1b" + tg])
            A("act", lambda h: h.activation(out=junk[:], in_=xi[:], func=AF.Square, bias=st1[:, 1:2], accum_out=st1[:, 2:3]),
              reads=[xk, "st1b" + tg], writes=["junk" + tg, "st1c" + tg])
            A("act", lambda h: h.activation(out=st1[:, 3:4], in_=st1[:, 2:3], func=AF.Sqrt, scale=1.0 / D, bias=epst[:, 0:1]),
              reads=["st1c" + tg, "epst" + tg], writes=["st1d" + tg])
            A("dve", lambda h: h.reciprocal(out=st1[:, 4:5], in_=st1[:, 3:4]), reads=["st1d" + tg], writes=["st1e" + tg])
            A("dve", lambda h: h.tensor_scalar(out=xi[:], in0=xi[:], scalar1=st1[:, 1:2], scalar2=st1[:, 4:5], op0=ALU.add, op1=ALU.mult),
              reads=[xk, "st1b" + tg, "st1e" + tg], writes=[xk])
            A("dve", lambda h: h.tensor_tensor(out=xi[:], in0=xi[:], in1=lng[:], op=ALU.mult), reads=[xk, "lng" + tg], writes=[xk])
            A("dve", lambda h: h.tensor_tensor(out=xi[:], in0=xi[:], in1=lnb[:], op=ALU.add), reads=[xk, "lnb" + tg], writes=[xk])

        for st in range(NSUP):
            T0 = st * TS2
            h1src = [xin[i_][:] for i_ in range(NSUB)]
            xTb = xT[st % 2]
            xTk = "xT%d" % (st % 2)
            xTb2 = xT[(st + 1) % 2]
            xTk2 = "xT%d" % ((st + 1) % 2)
            load_transpose(lambda sub, T0=T0: xo[T0 + sub * 128:T0 + (sub + 1) * 128, :], NSUB, xTb, xTk)
            A("sp", lambda h, T0=T0: h.dma_start(out=oat[:], in_=OT[:, :, T0:T0 + TS2].rearrange("a p t -> p a t")),
              reads=[("OT", j) for j in range(12)], writes=["oat"], dma=True)
            for dg in range(NDG):
                wga, wgak = load_wb(WGb[:, dg * DG:(dg + 1) * DG], KC, DG, "WG")
                wgb, wgbk = load_wb(WGb[:, D + dg * DG:D + (dg + 1) * DG], KC, DG, "WG")
                A("sp", lambda h, dg=dg: h.dma_start(out=wsm_a[:], in_=WAb[:, dg * DG:(dg + 1) * DG].rearrange("(kc p) n -> p kc n", p=128)),
                  reads=wb_keys["WA"], writes=["wsm_a"], dma=True)
                A("sp", lambda h, dg=dg: h.dma_start(out=wsm_b[:], in_=WBb[:, dg * DG:(dg + 1) * DG].rearrange("(kc p) n -> p kc n", p=128)),
                  reads=wb_keys["WB"], writes=["wsm_b"], dma=True)
                for c in range(DG // 128):
                    dc = dg * (DG // 128) + c
                    cs_ = slice(c * 128, (c + 1) * 128)
                    for kc in range(KC):
                        A("pe", lambda h, kc=kc, cs_=cs_, wga=wga, xTb=xTb: h.matmul(PS[2][:, 0:TS2], lhsT=wga[:, kc, cs_], rhs=xTb[:, kc, :], start=(kc == 0), stop=(kc == KC - 1)),
                          reads=[wgak, xTk], writes=[PK[2]])
                    for kc in range(KC):
                        A("pe", lambda h, kc=kc, cs_=cs_, wgb=wgb, xTb=xTb: h.matmul(PS[3][:, 0:TS2], lhsT=wgb[:, kc, cs_], rhs=xTb[:, kc, :], start=(kc == 0), stop=(kc == KC - 1)),
                          reads=[wgbk, xTk], writes=[PK[3]])
                    for kc in range(4):
                        A("pe", lambda h, kc=kc, cs_=cs_: h.matmul(PS[4][:, 0:TS2], lhsT=wsm_a[:, kc, cs_], rhs=oat[:, kc, :], start=(kc == 0), stop=(kc == 3)),
                          reads=["wsm_a", "oat"], writes=[PK[4]])
                    for kc in range(8):
                        A("pe", lambda h, kc=kc, cs_=cs_: h.matmul(PS[5][:, 0:TS2], lhsT=wsm_b[:, kc, cs_], rhs=oat[:, 4 + kc, :], start=(kc == 0), stop=(kc == 7)),
                          reads=["wsm_b", "oat"], writes=[PK[5]])
                    A("act", lambda h: h.activation(out=sg[0][:, 0:TS2], in_=PS[2][:, 0:TS2], func=AF.Sigmoid), reads=[PK[2]], writes=["sg0"])
                    A("act", lambda h: h.activation(out=sg[1][:, 0:TS2], in_=PS[3][:, 0:TS2], func=AF.Sigmoid), reads=[PK[3]], writes=["sg1"])
                    A("dve", lambda h: h.tensor_tensor(out=mt[0][:, 0:TS2], in0=sg[0][:, 0:TS2], in1=PS[4][:, 0:TS2], op=ALU.mult), reads=["sg0", PK[4]], writes=["mt0"])
                    A("dve", lambda h: h.tensor_tensor(out=mt[1][:, 0:TS2], in0=sg[1][:, 0:TS2], in1=PS[5][:, 0:TS2], op=ALU.mult), reads=["sg1", PK[5]], writes=["mt1"])
                    A("dve", lambda h, dc=dc: h.tensor_tensor(out=mergedT[:, dc, :], in0=mt[0][:, 0:TS2], in1=mt[1][:, 0:TS2], op=ALU.add),
                      reads=["mt0", "mt1"], writes=["mq"])
            for dg in range(NDG):
                wo, wok = load_wb(WOb[:, dg * DG:(dg + 1) * DG], KC, DG, "WO")
                for sub in range(NSUB):
                    pb = 6 + sub % 2
                    for kc in range(KC):
                        A("pe", lambda h, pb=pb, kc=kc, sub=sub, wo=wo: h.matmul(PS[pb][:, 0:DG], lhsT=mergedT[:, kc, sub * 128:(sub + 1) * 128], rhs=wo[:, kc, :],
                                                                         start=(kc == 0), stop=(kc == KC - 1)), reads=["mq", wok], writes=[PK[pb]])
                    xi = xin[sub]
                    xk = "xin%d" % sub
                    A("dve", lambda h, pb=pb, xi=xi, dg=dg: h.scalar_tensor_tensor(
                        out=xi[:, dg * DG:(dg + 1) * DG], in0=xi[:, dg * DG:(dg + 1) * DG], scalar=alpha, in1=PS[pb][:, 0:DG],
                        op0=ALU.mult, op1=ALU.add), reads=[PK[pb], xk], writes=[xk])
            for sub in range(NSUB):
                layer_norm(xin[sub], "xin%d" % sub, lngA, lnbA, junkA, st1A, epstA, "A")
            load_transpose(None, NSUB, xTb2, xTk2, xin_list=xin, xkeys=["xin%d" % i for i in range(NSUB)])
            for g4 in range(4):
                wqv, wqk = load_wb(WQb[:, g4 * 512:(g4 + 1) * 512], KC, 512, "WQ")
                for c in range(4):
                    hc = g4 * 4 + c
                    pb = 2 + hc % 2
                    for kc in range(KC):
                        A("pe", lambda h, pb=pb, kc=kc, c=c, wqv=wqv, xTb2=xTb2: h.matmul(PS[pb][:, 0:TS2], lhsT=wqv[:, kc, c * 128:(c + 1) * 128], rhs=xTb2[:, kc, :],
                                                                           start=(kc == 0), stop=(kc == KC - 1)), reads=[wqk, xTk2], writes=[PK[pb]])
                    A("act", lambda h, pb=pb, hc=hc: h.activation(out=qpT[:, hc, :], in_=PS[pb][:, 0:TS2], func=AF.Copy), reads=[PK[pb]], writes=["mq"])
            for sub in range(NSUB):
                gb_ = pstg[:, sub * PLE:(sub + 1) * PLE]
                gk_ = "pstg"
                A("sp", lambda h, gb_=gb_, sub=sub, T0=T0: h.dma_start(out=gb_, in_=pin[T0 + sub * 128:T0 + (sub + 1) * 128, :]), writes=[gk_], dma=True)
                for j in range(2):
                    A("pe", lambda h, gb_=gb_, j=j: h.transpose(out=PS[4][:, j * 128:(j + 1) * 128], in_=gb_[:, j * 128:(j + 1) * 128], identity=identf[:]),
                      reads=[gk_, "identf"], writes=[PK[4]])
                A("act", lambda h, sub=sub: h.activation(out=ppT[:, :, sub * 128:(sub + 1) * 128], in_=PS[4][:, 0:256].rearrange("p (a b) -> p a b", a=2), func=AF.Copy),
                  reads=[PK[4]], writes=["ppT"])
            for dg in range(NDG):
                wg_, wgk_ = load_wb(WPGb[:, dg * DG:(dg + 1) * DG], KC, DG, "WPG")
                for sub in range(NSUB):
                    pb = 6 + sub % 2
                    pb2 = 4 + sub % 2
                    for kc in range(KC):
                        A("pe", lambda h, pb=pb, kc=kc, sub=sub, wg_=wg_, xTb2=xTb2: h.matmul(PS[pb][:, 0:DG], lhsT=xTb2[:, kc, sub * 128:(sub + 1) * 128], rhs=wg_[:, kc, :],
                                                                           start=(kc == 0), stop=(kc == KC - 1)), reads=[xTk2, wgk_], writes=[PK[pb]])
                    for kc in range(2):
                        A("pe", lambda h, pb2=pb2, kc=kc, sub=sub, dg=dg: h.matmul(PS[pb2][:, 0:DG], lhsT=ppT[:, kc, sub * 128:(sub + 1) * 128], rhs=wpp[:, kc, dg * DG:(dg + 1) * DG],
                                                                            start=(kc == 0), stop=(kc == 1)), reads=["ppT", "wpp"], writes=[PK[pb2]])
                    sgi = sg[sub % 2]
                    sgk = "sg%d" % (sub % 2)
                    mti = mt[sub % 2]
                    mtk = "mt%d" % (sub % 2)
                    A("act", lambda h, pb=pb, sgi=sgi: h.activation(out=sgi[:, 0:DG], in_=PS[pb][:, 0:DG], func=AF.Sigmoid), reads=[PK[pb]], writes=[sgk])
                    A("dve", lambda h, pb2=pb2, sgi=sgi, mti=mti: h.tensor_tensor(out=mti[:, 0:DG], in0=sgi[:, 0:DG], in1=PS[pb2][:, 0:DG], op=ALU.mult),
                      reads=[sgk, PK[pb2]], writes=[mtk])
                    xi = xin[sub]
                    A("dve", lambda h, xi=xi, mti=mti, dg=dg: h.scalar_tensor_tensor(
                        out=mti[:, 0:DG], in0=xi[:, dg * DG:(dg + 1) * DG], scalar=alpha, in1=mti[:, 0:DG], op0=ALU.mult, op1=ALU.add),
                      reads=["xin%d" % sub, mtk], writes=[mtk])
                    A("sp", lambda h, mti=mti, sub=sub, dg=dg, T0=T0: h.dma_start(out=Y2[T0 + sub * 128:T0 + (sub + 1) * 128, dg * DG:(dg + 1) * DG], in_=mti[:, 0:DG]),
                      reads=[mtk], writes=[("Y2", st, sub, dg)], dma=True)
            A("sp", lambda h, T0=T0: h.dma_start(out=QPs[:, :, T0:T0 + TS2], in_=qpT), reads=["mq"], writes=[("QPs", st)], dma=True)
            for sub in range(NSUB):
                A("sp", lambda h, sub=sub, T0=T0, h1src=h1src: h.dma_start(out=H1s[T0 + sub * 128:T0 + (sub + 1) * 128, :], in_=h1src[sub]),
                  reads=["xin%d" % sub], writes=[("H1s", st, sub)], dma=True)
            emit_cast((N_EXP // CAST_ROWS * 2 + NSUP - 1) // NSUP)
        emit_cast(100000)
        flush_cast_store()
        assert not cast_pending
        areset()
        xin[:] = [sb("xinB%d" % i, [128, D], F32) for i in range(NSUB)]
        qpB = sb("qpB", [128, 16, TS2], BF16)
        skT = sb("skT", [128, 16, 128], BF16)
        lngB = sb("lngB", [128, D], F32)
        lnbB = sb("lnbB", [128, D], F32)
        junk = sb("junk", [128, D], BF16)
        junk2 = [junk, None]
        st1B = sb("st1B", [128, 8], F32)
        epstB = sb("epstB", [128, 8], F32)
        A("dve", lambda h: h.memset(epstB[:], LN_EPS), writes=["epstB"])
        s_sb = sb("s_sb", [128, 16, 128], F32)
        s2 = sb("s2", [128, 256], F32)
        m16 = sb("m16", [128, 16, 16], F32)
        i16 = sb("i16", [128, 16, 16], U32)
        i16f = sb("i16f", [128, 16, 16], F32)
        cand = sb("cand", [128, 8, 256], F32)
        junk2[1] = cand[:].rearrange("p a b -> p (a b)").bitcast(BF16)[:, 0:D]
        b16 = sb("b16", [128, 8, 16], F32)
        c16 = sb("c16", [128, 8, 16], U32)
        cfa = sb("cfa", [128, 8, 16], F32)
        cfb = sb("cfb", [128, 8, 16], F32)
        oh = s_sb[:].rearrange("p a b -> p (a b)").rearrange("p (a b c) -> p a b c", a=8, b=16)
        ei = sb("ei", [128, 8, 16], F32)
        ei2 = sb("ei2", [128, 8, 16], F32)
        eidx_l = [sb("eidx%d" % i, [128, 128], U32) for i in range(NSUB)]
        gate_l = [sb("gate%d" % i, [128, 8, 16], F32) for i in range(NSUB)]
        zs = sb("zs", [128, 8], F32)
        actp = sb("actp", [128, 128], F32)
        wgt = sb("wgt", [128, 128], F32)
        NG = 10
        gbuf = [sb("gbuf%d" % i, [128, 2, D], BF16) for i in range(NG)]
        diag = [sb("diag%d" % i, [128, 128], BF16) for i in range(3)]
        ffn = sb("ffn", [128, D], F32)
        A("sp", lambda h: h.dma_start(out=lngB[:], in_=ln2_g.partition_broadcast(128)), writes=["lngB"], dma=True)
        A("sp", lambda h: h.dma_start(out=lnbB[:], in_=ln2_b.partition_broadcast(128)), writes=["lnbB"], dma=True)
        for hc in range(16):
            xi = xin[hc % len(xin)]
            xk = "xin%d" % (hc % len(xin))
            A("sp", lambda h, xi=xi, hc=hc: h.dma_start(out=xi[:, 0:128], in_=peer_sk[hc, :, :]), writes=[xk], dma=True)
            A("pe", lambda h, xi=xi: h.transpose(out=PS[0][:, 0:128], in_=xi[:, 0:128], identity=identf[:]),
              reads=[xk, "identf"], writes=[PK[0]])
            A("act", lambda h, hc=hc: h.activation(out=skT[:, hc, :], in_=PS[0][:, 0:128], func=AF.Copy), reads=[PK[0]], writes=["skT"])

        for st in range(NSUP):
            T0 = st * TS2
            A("sp", lambda h, T0=T0: h.dma_start(out=qpB[:], in_=QPs[:, :, T0:T0 + TS2]), reads=[("QPs", st)], writes=["qpB"], dma=True)
            for sub in range(NSUB):
                A("sp", lambda h, sub=sub, T0=T0, xi=xin[sub]: h.dma_start(out=xi[:], in_=H1s[T0 + sub * 128:T0 + (sub + 1) * 128, :]),
                  reads=[("H1s", st, sub)], writes=["xin%d" % sub], dma=True)
            def peer_select(sub, gate, eidx, gk, ek):
                xi = xin[sub]
                xk = "xin%d" % sub
                for hc in range(16):
                    bk = hc // 4
                    A("pe", lambda h, hc=hc, bk=bk, sub=sub: h.matmul(PS[bk][:, (hc % 4) * 128:(hc % 4 + 1) * 128], lhsT=qpB[:, hc, sub * 128:(sub + 1) * 128], rhs=skT[:, hc, :],
                                                                  start=True, stop=True), reads=["qpB", "skT"], writes=[PK[bk]])
                for bk in range(4):
                    A("act", lambda h, bk=bk: h.activation(out=s_sb[:, bk * 4:(bk + 1) * 4, :], in_=PS[bk][:, :].rearrange("p (a b) -> p a b", a=4), func=AF.Copy),
                      reads=[PK[bk]], writes=[("s_sb", bk)])
                for hc in range(16):
                    sk_ = ("s_sb", hc // 4)
                    A("dve", lambda h, hc=hc: h.max(out=m16[:, hc, 0:8], in_=s_sb[:, hc, :]), reads=[sk_], writes=["m16"])
                    A("dve", lambda h, hc=hc: h.max_index(out=i16[:, hc, 0:8], in_max=m16[:, hc, 0:8], in_values=s_sb[:, hc, :]), reads=[sk_, "m16"], writes=["i16"])
                    A("dve", lambda h, hc=hc: h.match_replace(out=s2[:, 0:128], in_to_replace=m16[:, hc, 0:8], in_values=s_sb[:, hc, :], imm_value=-1e30),
                      reads=[sk_, "m16"], writes=["s2"])
                    A("dve", lambda h, hc=hc: h.max(out=m16[:, hc, 8:16], in_=s2[:, 0:128]), reads=["s2"], writes=["m16"])
                    A("dve", lambda h, hc=hc: h.max_index(out=i16[:, hc, 8:16], in_max=m16[:, hc, 8:16], in_values=s2[:, 0:128]), reads=["s2", "m16"], writes=["i16"])
                A("dve", lambda h: h.tensor_copy(out=i16f[:], in_=i16[:]), reads=["i16"], writes=["i16f"])
                m16r = m16[:].rearrange("p (h c) k -> p h c k", c=2)
                i16r = i16f[:].rearrange("p (h c) k -> p h c k", c=2)
                A("dve", lambda h: h.tensor_tensor(
                    out=cand[:].rearrange("p h (a b) -> p h a b", a=16),
                    in0=m16r[:, :, 0, :].unsqueeze(3).broadcast_to([128, 8, 16, 16]),
                    in1=m16r[:, :, 1, :].unsqueeze(2).broadcast_to([128, 8, 16, 16]), op=ALU.add), reads=["m16"], writes=["cand"])
                for hh in range(8):
                    A("dve", lambda h, hh=hh: h.max(out=b16[:, hh, 0:8], in_=cand[:, hh, :]), reads=["cand"], writes=["b16"])
                    A("dve", lambda h, hh=hh: h.max_index(out=c16[:, hh, 0:8], in_max=b16[:, hh, 0:8], in_values=cand[:, hh, :]), reads=["cand", "b16"], writes=["c16"])
                    A("dve", lambda h, hh=hh: h.match_replace(out=s2[:], in_to_replace=b16[:, hh, 0:8], in_values=cand[:, hh, :], imm_value=-1e30),
                      reads=["cand", "b16"], writes=["s2"])
                    A("dve", lambda h, hh=hh: h.max(out=b16[:, hh, 8:16], in_=s2[:]), reads=["s2"], writes=["b16"])
                    A("dve", lambda h, hh=hh: h.max_index(out=c16[:, hh, 8:16], in_max=b16[:, hh, 8:16], in_values=s2[:]), reads=["s2", "b16"], writes=["c16"])
                A("dve", lambda h: h.tensor_tensor(out=gate[:], in0=b16[:], in1=b16[:, :, 0:1].broadcast_to([128, 8, 16]), op=ALU.subtract),
                  reads=["b16"], writes=[gk])
                A("act", lambda h: h.activation(out=gate[:], in_=gate[:], func=AF.Exp), reads=[gk], writes=[gk])
                A("dve", lambda h: h.reduce_sum(out=zs[:], in_=gate[:], axis=AX.X), reads=[gk], writes=["zs"])
                A("dve", lambda h: h.reciprocal(out=zs[:], in_=zs[:]), reads=["zs"], writes=["zs"])
                A("dve", lambda h: h.tensor_tensor(out=gate[:], in0=gate[:], in1=zs[:].unsqueeze(2).broadcast_to([128, 8, 16]), op=ALU.mult),
                  reads=[gk, "zs"], writes=[gk])
                A("dve", lambda h: h.tensor_copy(out=cfa[:], in_=c16[:]), reads=["c16"], writes=["cfa"])
                A("dve", lambda h: h.tensor_tensor(out=oh[:], in0=cfa[:].unsqueeze(3).broadcast_to([128, 8, 16, 16]),
                                                   in1=thr16.unsqueeze(1).unsqueeze(1).broadcast_to([128, 8, 16, 16]), op=ALU.is_ge),
                  reads=["cfa", "iota16"], writes=[("s_sb", 0), ("s_sb", 1), ("s_sb", 2), ("s_sb", 3)])
                A("dve", lambda h: h.reduce_sum(out=cfb[:], in_=oh[:], axis=AX.X), reads=[("s_sb", 0), ("s_sb", 1), ("s_sb", 2), ("s_sb", 3)], writes=["cfb"])
                A("dve", lambda h: h.scalar_tensor_tensor(out=cfa[:], in0=cfb[:], scalar=-16.0, in1=cfa[:], op0=ALU.mult, op1=ALU.add),
                  reads=["cfa", "cfb"], writes=["cfa"])
                iob = iota16[:, 0, :].unsqueeze(1).unsqueeze(1).broadcast_to([128, 8, 16, 16])
                for (cf, cfk, cidx, eo, eok) in ((cfb, "cfb", 0, ei, "ei"), (cfa, "cfa", 1, ei2, "ei2")):
                    A("dve", lambda h, cf=cf: h.tensor_tensor(out=oh[:], in0=iob, in1=cf[:].unsqueeze(3).broadcast_to([128, 8, 16, 16]), op=ALU.is_equal),
                      reads=[cfk, "iota16"], writes=[("s_sb", 0), ("s_sb", 1), ("s_sb", 2), ("s_sb", 3)])
                    A("dve", lambda h, cidx=cidx: h.tensor_tensor(out=oh[:], in0=oh[:], in1=i16r[:, :, cidx, :].unsqueeze(2).broadcast_to([128, 8, 16, 16]), op=ALU.mult),
                      reads=[("s_sb", 0), ("s_sb", 1), ("s_sb", 2), ("s_sb", 3), "i16f"], writes=[("s_sb", 0), ("s_sb", 1), ("s_sb", 2), ("s_sb", 3)])
                    A("dve", lambda h, eo=eo: h.reduce_sum(out=eo[:], in_=oh[:], axis=AX.X), reads=[("s_sb", 0), ("s_sb", 1), ("s_sb", 2), ("s_sb", 3)], writes=[eok])
                A("dve", lambda h: h.scalar_tensor_tensor(out=ei[:], in0=ei[:], scalar=128.0, in1=ei2[:], op0=ALU.mult, op1=ALU.add),
                  reads=["ei", "ei2"], writes=["ei"])
                A("dve", lambda h: h.tensor_copy(out=eidx[:], in_=ei[:].rearrange("p a b -> p (a b)")), reads=["ei"], writes=[ek])
            def peer_slots_fn(sub, gate, eidx, gk, ek):
                xi = xin[sub]
                xk = "xin%d" % sub
                def _unused():
                    pass
                A("dve", lambda h: h.memset(actp[:], 0.0), writes=["actp"])
                A("sp", lambda h, sub=sub, T0=T0: h.dma_start(out=ffn[:], in_=Y2[T0 + sub * 128:T0 + (sub + 1) * 128, :]),
                  reads=[("Y2", st, sub, dg) for dg in range(NDG)], writes=["ffn"], dma=True)
                NBK = D // 512
                gflat = gate[:].rearrange("p a b -> p (a b)")
                pend = []

                def emit_pv(s_, gb_, gk_):
                    dg_ = diag[s_ % 3]
                    dk_ = "diag%d" % (s_ % 3)
                    A("dve", lambda h: h.tensor_scalar(
                        out=dg_[:], in0=identf[:], scalar1=wgt[:, s_:s_ + 1], scalar2=gflat[:, s_:s_ + 1], op0=ALU.mult, op1=ALU.mult),
                      reads=[("wgt", s_), gk, "identf"], writes=[dk_])
                    for j in range(NBK):
                        A("pe", lambda h, j=j: h.matmul(
                            PS[4 + j][:, :], lhsT=dg_[:], rhs=gb_[:, 1, j * 512:(j + 1) * 512], start=(s_ == 0), stop=(s_ == peer_slots - 1)),
                          reads=[dk_, gk_], writes=[PK[4 + j]])
                for s_ in range(peer_slots):
                    gi_ = gcnt[0] % NG
                    gcnt[0] += 1
                    gb_ = gbuf[gi_]
                    gk_ = "gbuf%d" % gi_
                    A("pool", lambda h, gb_=gb_, s_=s_: h.indirect_dma_start(
                        out=gb_[:].rearrange("p a d -> p (a d)"), out_offset=None, in_=UVB.rearrange("e a d -> e (a d)"),
                        in_offset=bass.IndirectOffsetOnAxis(ap=eidx[:, s_:s_ + 1], axis=0)),
                      reads=[ek] + (uvb_keys if s_ == 0 else []), writes=[gk_], dma=True)
                    A("dve", lambda h, gb_=gb_, s_=s_, xi=xi: h.scalar_tensor_tensor(
                        out=junk2[s_ % 2], in0=gb_[:, 0, :], scalar=1.0, in1=xi[:], op0=ALU.mult, op1=ALU.mult, accum_out=actp[:, s_:s_ + 1]),
                      reads=[gk_, xk, "actp"], writes=[("actp", s_), ("junkB" if s_ % 2 == 0 else "cand")])
                    A("act", lambda h, s_=s_: h.activation(out=wgt[:, s_:s_ + 1], in_=actp[:, s_:s_ + 1], func=AF.Gelu),
                      reads=[("actp", s_)], writes=[("wgt", s_)])
                    pend.append((s_, gb_, gk_))
                    if len(pend) > 1:
                        emit_pv(*pend.pop(0))
                while pend:
                    emit_pv(*pend.pop(0))
                for j in range(NBK):
                    A("dve", lambda h, j=j: h.tensor_tensor(out=ffn[:, j * 512:(j + 1) * 512], in0=ffn[:, j * 512:(j + 1) * 512], in1=PS[4 + j][:, :], op=ALU.add),
                      reads=["ffn", PK[4 + j]], writes=["ffn"])
                layer_norm(ffn, "ffn", lngB, lnbB, junk, st1B, epstB, "B")
                out_ops.append(A("sp", lambda h, sub=sub, T0=T0: h.dma_start(out=out[T0 + sub * 128:T0 + (sub + 1) * 128, :], in_=ffn[:]),
                                 reads=["ffn"], dma=True))
            for sub in range(NSUB):
                peer_select(sub, gate_l[sub], eidx_l[sub], "gate%d" % sub, "eidx%d" % sub)
            for sub in range(NSUB):
                peer_slots_fn(sub, gate_l[sub], eidx_l[sub], "gate%d" % sub, "eidx%d" % sub)
        S.finals = out_ops

        sems = {e: es.enter_context(nc.semaphore("s_" + e)) for e in ENG_NAMES}
        dsems = [es.enter_context(nc.semaphore("d%d" % i)) for i in range(N_DMA_SEMS)]
        block = es.enter_context(nc.Block())

        def mk(e):
            def body(h):
                S.emit_engine(e, h, sems, dsems)
            return body

        block.tensor(mk("pe"))
        block.scalar(mk("act"))
        block.vector(mk("dve"))
        block.gpsimd(mk("pool"))
        block.sync(mk("sp"))
    return nc


def make_consts(pos0, halo_valid):
    bf = ml_dtypes.bfloat16
    NEG = -30000.0
    k = np.arange(128)[:, None]
    q = np.arange(128)[None, :]
    kill = 0.0 if halo_valid else NEG
    prevA = np.where(k >= q, 0.0, NEG)
    cur = np.where(k <= q, 0.0, NEG)
    prevB = np.where(k > q, 0.0, NEG)
    maskA = np.stack([prevA, cur, np.minimum(prevA, kill)], axis=1).astype(np.float32)
    mB = np.stack([prevB, cur, np.minimum(prevB, kill)], axis=1)
    maskB = np.tile(mB, (1, 1, 4)).astype(np.float32)
    ident = np.eye(128, dtype=np.float32)
    rot = np.zeros((128, 128), np.float32)
    for m in range(128):
        rot[(m + 64) % 128, m] = 1.0
    cbf = np.stack([ident, rot, np.ones((128, 128), np.float32)], axis=1)
    inv = 1.0 / (10000.0 ** (np.arange(0, 128, 2, dtype=np.float32) / 128.0))
    pos = np.maximum(pos0 - HALO + np.arange(CTX), 0).astype(np.float32)
    ang = pos[None, :] * inv[:, None].astype(np.float32)
    cos = np.cos(ang).astype(np.float32)
    sin = np.sin(ang).astype(np.float32)
    cosT = np.concatenate([cos, cos], axis=0)
    sinT = np.concatenate([-sin, sin], axis=0)
    cs = np.stack([cosT, sinT], axis=1)
    io = np.arange(16, dtype=np.float32)
    iota = np.tile(np.stack([io, 16.0 * (io + 1.0)], axis=0)[None], (128, 1, 1)).astype(np.float32)
    return {
        "c_identf": ident,
        "c_bf": cbf.astype(bf),
        "c_maskA": maskA.astype(bf),
        "c_maskB": maskB.astype(bf),
        "c_cs": cs.astype(bf),
        "c_iota": iota,
    }


_PROG = {}


def kernel(x, p, w_in, sinks, w_branch_a, w_branch_b, w_out, ln1_g, ln1_b, peer_wq, peer_subkeys,
           peer_u, peer_v, ple_gate, ple_proj, ln2_g, ln2_b):
    x = np.asarray(x)
    Bn, Sq, D = x.shape
    depth = 1
    alpha = (2 * depth) ** 0.25
    n_cores = 8
    assert Bn * Sq == n_cores * TOWN and Sq == 2 * TOWN
    if D not in _PROG:
        _PROG[D] = build_program(D, alpha)
    nc = _PROG[D]
    f = lambda a: np.ascontiguousarray(np.asarray(a, dtype=np.float32))
    shared = {
        "w_in": f(w_in)[0], "sinks": f(sinks)[0].reshape(1, 8), "w_branch_a": f(w_branch_a)[0], "w_branch_b": f(w_branch_b)[0],
        "w_out": f(w_out)[0], "ln1_g": f(ln1_g)[0].reshape(1, D), "ln1_b": f(ln1_b)[0].reshape(1, D),
        "peer_wq": f(peer_wq)[0], "peer_subkeys": f(peer_subkeys)[0].reshape(16, 128, 128),
        "peer_u": f(peer_u)[0], "peer_v": f(peer_v)[0], "ple_gate": f(ple_gate)[0], "ple_proj": f(ple_proj)[0],
        "ln2_g": f(ln2_g)[0].reshape(1, D), "ln2_b": f(ln2_b)[0].reshape(1, D),
    }
    xf = f(x)
    pf = f(p)[0]
    in_maps = []
    for c in range(n_cores):
        b, half = c // 2, c % 2
        t0 = half * TOWN
        m = dict(shared)
        m["xo"] = np.ascontiguousarray(xf[b, t0:t0 + TOWN])
        m["xh"] = np.ascontiguousarray(xf[b, 0:HALO]) if half == 1 else np.zeros((HALO, D), np.float32)
        m["p"] = np.ascontiguousarray(pf[b, t0:t0 + TOWN])
        m.update(make_consts(t0, half == 1))
        in_maps.append(m)
    res = run_bass_kernel_spmd(nc, in_maps, core_ids=list(range(n_cores)))
    outs = [np.asarray(r["out"]) for r in res.results]
    full = np.zeros((Bn, Sq, D), np.float32)
    for c in range(n_cores):
        b, half = c // 2, c % 2
        full[b, half * TOWN:(half + 1) * TOWN] = outs[c]
    return full
```
